# Optimizing a Trainium2 kernel written in Bass

```python
import jax, jax.numpy as jnp
from jax import lax
import numpy as np

D_MODEL = 1024
BATCH = 32
SEQ = 256
DEPTH = 2
DEC_BATCH = 4
DEC_SEQ = 1024
PAST_LEN = 256

GRID_W = 64
CHUNK = 16
N_EVEN = (DEPTH + 1) // 2
N_ODD = DEPTH // 2
EPS = 1e-6
H_A = 4
DK_A = 128
DV_A = 128
H_B = 4
DK_B = 64
DV_B = 128
GLA_RANK = 16
GLA_GATE_NORM = 16.0
H_C = 8
DK_C = 128
DV_C = 128
ROPE_BASE = 10000.0

W_A = H_A * DV_A
W_B = H_B * DV_B
W_EVEN = W_A + W_B
W_ODD = H_C * DV_C
EVEN_SPLITS = [H_A * DK_A, W_A, H_A * DK_A, H_A * DK_A, W_A, H_B * DK_B, H_B * DK_B, W_B, W_B, GLA_RANK, GLA_RANK]
EVEN_SPLIT_IDX = np.cumsum(EVEN_SPLITS)[:-1].tolist()
D_IN_EVEN = int(sum(EVEN_SPLITS))
ODD_SPLIT_IDX = [H_C * DK_C, 2 * H_C * DK_C, 2 * H_C * DK_C + W_ODD]
D_IN_ODD = 2 * H_C * DK_C + 2 * W_ODD

kernel_name = "hybrid_hgrn2_gla_retention_diffusion_step"


def rms_norm(x, w):
    xf = x.astype(jnp.float32)
    y = xf * lax.rsqrt(jnp.mean(xf * xf, axis=-1, keepdims=True) + EPS)
    return (y * w.astype(jnp.float32)).astype(x.dtype)


def group_rms_norm(o, w):
    b, t, h, v = o.shape
    of = o.astype(jnp.float32)
    y = of * lax.rsqrt(jnp.mean(of * of, axis=-1, keepdims=True) + EPS)
    return y.reshape(b, t, h * v) * w.astype(jnp.float32)


def grid_angles(T):
    rows = T // GRID_W
    t_row = jnp.repeat(jnp.arange(rows), GRID_W).astype(jnp.float32)
    t_col = jnp.tile(jnp.arange(GRID_W), rows).astype(jnp.float32)
    half = DK_C // 2
    inv = ROPE_BASE ** (-jnp.arange(0, half, 2, dtype=jnp.float32) / half)
    ang_r = t_row[:, None] * inv
    ang_c = t_col[:, None] * inv
    ang = jnp.concatenate([ang_r, ang_r, ang_c, ang_c], axis=-1)
    return jnp.cos(ang), jnp.sin(ang)


def apply_grid_rope(x, cos, sin):
    half = DK_C // 2
    qd = half // 2
    def rot(a):
        return jnp.concatenate([-a[..., qd:], a[..., :qd]], axis=-1)
    xr = jnp.concatenate([rot(x[..., :half]), rot(x[..., half:])], axis=-1)
    return x * cos[None, :, None, :] + xr * sin[None, :, None, :]


def chunk_gated_linear(q, k, v, log_g, s0):
    f32 = jnp.float32
    q, k, v, log_g, s0 = (a.astype(f32) for a in (q, k, v, log_g, s0))
    b_, t, h, kd = q.shape
    n = t // CHUNK
    rs = lambda a: a.reshape(b_, n, CHUNK, h, a.shape[-1])
    qc, kc, vc, gc = rs(q), rs(k), rs(v), rs(log_g)
    bcum = jnp.cumsum(gc, axis=2)
    b_last = bcum[:, :, -1]
    causal = jnp.tril(jnp.ones((CHUNK, CHUNK), dtype=bool))
    diff = bcum[:, :, :, None] - bcum[:, :, None, :]
    decay = jnp.exp(jnp.where(causal[None, None, :, :, None, None], diff, -jnp.inf))
    if log_g.shape[-1] == 1:
        scores = jnp.einsum('bnthk,bnshk->bntsh', qc, kc) * decay[..., 0]
    else:
        scores = jnp.einsum('bnthk,bnshk,bntshk->bntsh', qc, kc, decay)
    o_intra = jnp.einsum('bntsh,bnshv->bnthv', scores, vc)
    q_dec = qc * jnp.exp(bcum)
    k_dec = kc * jnp.exp(b_last[:, :, None] - bcum)
    chunk_kv = jnp.einsum('bnshk,bnshv->bnhkv', k_dec, vc)
    g_chunk = jnp.exp(b_last)

    def step(S, inp):
        g_n, kv_n = inp
        return g_n[..., None] * S + kv_n, S

    s_final, s_prev = lax.scan(step, s0, (jnp.moveaxis(g_chunk, 1, 0), jnp.moveaxis(chunk_kv, 1, 0)))
    s_prev = jnp.moveaxis(s_prev, 0, 1)
    o_inter = jnp.einsum('bnthk,bnhkv->bnthv', q_dec, s_prev)
    o = (o_intra + o_inter).reshape(b_, t, h, v.shape[-1])
    return o, s_final


def bidir_scan(q, k_pair, v, lg_pair, s0):
    flip = lambda a: a[:, ::-1]
    o_f, s_f = chunk_gated_linear(q, k_pair[0], v, lg_pair[0], s0[:, 0])
    o_b, s_b = chunk_gated_linear(flip(q), flip(k_pair[1]), flip(v), flip(lg_pair[1]), s0[:, 1])
    return o_f + flip(o_b), jnp.stack([s_f, s_b], axis=1)


def even_mixer(h, s0_a, s0_b, lb, w_in, gk_w, gk_b, gn_w, w_out):
    b_, t, _ = h.shape
    p = h @ w_in
    aq, ai, af_f, af_b, ag, bq, bk, bv, bg, bl_f, bl_b = jnp.split(p, EVEN_SPLIT_IDX, axis=-1)
    hd = lambda a, nh: a.reshape(b_, t, nh, -1)
    lbf = lb.astype(jnp.float32)
    def log_forget(a):
        return jnp.logaddexp(jnp.log(lbf), jnp.log1p(-lbf) + jax.nn.log_sigmoid(a.astype(jnp.float32)))
    lf_f = hd(log_forget(af_f), H_A)
    lf_b = hd(log_forget(af_b), H_A)
    o_a, st_a = bidir_scan(hd(jax.nn.silu(aq), H_A), (-jnp.expm1(lf_f), -jnp.expm1(lf_b)),
                           hd(ai, H_A), (lf_f, lf_b), s0_a)
    def gla_log_gate(low, d):
        return jax.nn.log_sigmoid((low @ gk_w[d] + gk_b[d]).astype(jnp.float32)) / GLA_GATE_NORM
    kb = hd(bk, H_B)
    o_b, st_b = bidir_scan(hd(bq, H_B) * (DK_B ** -0.5), (kb, kb), hd(bv, H_B),
                           (hd(gla_log_gate(bl_f, 0), H_B), hd(gla_log_gate(bl_b, 1), H_B)), s0_b)
    o = jnp.concatenate([o_a, o_b], axis=2)
    o = group_rms_norm(o, gn_w).astype(h.dtype) * jax.nn.silu(jnp.concatenate([ag, bg], axis=-1))
    return o @ w_out, st_a, st_b


def odd_mixer(h, s0, decay_logit, w_in, gn_w, w_out, rope):
    b_, t, _ = h.shape
    p = h @ w_in
    q, k, v, g = jnp.split(p, ODD_SPLIT_IDX, axis=-1)
    q = q.reshape(b_, t, H_C, DK_C)
    k = k.reshape(b_, t, H_C, DK_C) * (DK_C ** -0.5)
    if rope is not None:
        q = apply_grid_rope(q, rope[0], rope[1])
        k = apply_grid_rope(k, rope[0], rope[1])
    lg = jax.nn.log_sigmoid(decay_logit.astype(jnp.float32))
    lg_f = jnp.broadcast_to(lg[0][:, None], (b_, t, H_C, 1))
    lg_b = jnp.broadcast_to(lg[1][:, None], (b_, t, H_C, 1))
    o, st = bidir_scan(q, (k, k), v.reshape(b_, t, H_C, DV_C), (lg_f, lg_b), s0)
    o = group_rms_norm(o, gn_w).astype(h.dtype) * jax.nn.silu(g)
    return o @ w_out, st


def setup_inputs(seed: int = 0) -> dict:
    key = jax.random.key(seed)
    ks = jax.random.split(key, 24)
    f32 = jnp.float32
    nrm = lambda k, shape, s: jax.random.normal(k, shape, f32) * s
    ret_init = jnp.asarray(np.log(2.0 ** (5.0 + np.arange(H_C)) - 1.0), dtype=f32)
    return {
        "x_prompt": nrm(ks[0], (BATCH, SEQ, D_MODEL), 1.0),
        "x_sample": nrm(ks[1], (DEC_BATCH, DEC_SEQ, D_MODEL), 1.0),
        "state_hgrn": nrm(ks[2], (DEC_BATCH, N_EVEN, 2, H_A, DK_A, DV_A), 0.5),
        "state_gla": nrm(ks[3], (DEC_BATCH, N_EVEN, 2, H_B, DK_B, DV_B), 0.5),
        "state_ret": nrm(ks[4], (DEC_BATCH, N_ODD, 2, H_C, DK_C, DV_C), 1.0),
        "c": nrm(ks[5], (DEC_BATCH, D_MODEL), 1.0),
        "c_ctx": nrm(ks[6], (D_MODEL,), 1.0),
        "norm_w": 1.0 + nrm(ks[7], (DEPTH, D_MODEL), 0.02),
        "ada_w": nrm(ks[8], (DEPTH, D_MODEL, 3 * D_MODEL), 0.5 * D_MODEL ** -0.5),
        "ada_b": nrm(ks[9], (DEPTH, 3 * D_MODEL), 0.02),
        "w_in_even": nrm(ks[10], (N_EVEN, D_MODEL, D_IN_EVEN), D_MODEL ** -0.5),
        "hgrn_lb": nrm(ks[11], (N_EVEN + 1, H_A * DK_A), 0.5),
        "gla_gk_w": nrm(ks[12], (N_EVEN, 2, GLA_RANK, H_B * DK_B), GLA_RANK ** -0.5),
        "gla_gk_b": nrm(ks[13], (N_EVEN, 2, H_B * DK_B), 0.02),
        "gn_even": 1.0 + nrm(ks[14], (N_EVEN, W_EVEN), 0.02),
        "w_out_even": nrm(ks[15], (N_EVEN, W_EVEN, D_MODEL), W_EVEN ** -0.5),
        "w_in_odd": nrm(ks[16], (N_ODD, D_MODEL, D_IN_ODD), D_MODEL ** -0.5),
        "ret_decay": ret_init[None, None, :] + nrm(ks[17], (N_ODD, 2, H_C), 0.1),
        "gn_odd": 1.0 + nrm(ks[18], (N_ODD, W_ODD), 0.02),
        "w_out_odd": nrm(ks[19], (N_ODD, W_ODD, D_MODEL), W_ODD ** -0.5),
        "final_norm_w": 1.0 + nrm(ks[20], (D_MODEL,), 0.02),
    }


def reference(x_prompt, x_sample, state_hgrn, state_gla, state_ret, c, c_ctx, norm_w, ada_w, ada_b,
              w_in_even, hgrn_lb, gla_gk_w, gla_gk_b, gn_even, w_out_even, w_in_odd, ret_decay, gn_odd,
              w_out_odd, final_norm_w):
    f32 = jnp.float32
    b_ctx = x_prompt.shape[0]
    lbs = jnp.cumsum(jax.nn.softmax(hgrn_lb.astype(f32), axis=0), axis=0)
    cond_ctx = jax.nn.silu(c_ctx)[None, None, :]
    cond_lat = jax.nn.silu(c)[:, None, :]
    rope = grid_angles(x_sample.shape[1])
    x_c, x_l = x_prompt, x_sample
    new_hgrn, new_gla, new_ret = [], [], []
    for l in range(DEPTH):
        i = l // 2
        sh_c, sc_c, g_c = jnp.split(cond_ctx @ ada_w[l] + ada_b[l], 3, axis=-1)
        sh_l, sc_l, g_l = jnp.split(cond_lat @ ada_w[l] + ada_b[l], 3, axis=-1)
        h_c = rms_norm(x_c, norm_w[l]) * (1.0 + sc_c) + sh_c
        h_l = rms_norm(x_l, norm_w[l]) * (1.0 + sc_l) + sh_l
        if l % 2 == 0:
            z_a = jnp.zeros((b_ctx, 2, H_A, DK_A, DV_A), f32)
            z_b = jnp.zeros((b_ctx, 2, H_B, DK_B, DV_B), f32)
            out_c, st_a, st_b = even_mixer(h_c, z_a, z_b, lbs[i], w_in_even[i], gla_gk_w[i], gla_gk_b[i],
                                           gn_even[i], w_out_even[i])
            out_l, _, _ = even_mixer(h_l, state_hgrn[:, i], state_gla[:, i], lbs[i], w_in_even[i],
                                     gla_gk_w[i], gla_gk_b[i], gn_even[i], w_out_even[i])
            new_hgrn.append(st_a)
            new_gla.append(st_b)
        else:
            z_c = jnp.zeros((b_ctx, 2, H_C, DK_C, DV_C), f32)
            out_c, st_c = odd_mixer(h_c, z_c, ret_decay[i], w_in_odd[i], gn_odd[i], w_out_odd[i], None)
            out_l, _ = odd_mixer(h_l, state_ret[:, i], ret_decay[i], w_in_odd[i], gn_odd[i], w_out_odd[i], rope)
            new_ret.append(st_c)
        x_c = x_c + g_c * out_c
        x_l = x_l + g_l * out_l
    y_prompt = rms_norm(x_c, final_norm_w)
    y_sample = rms_norm(x_l, final_norm_w)
    new_state_hgrn = jnp.stack(new_hgrn, axis=1)
    new_state_gla = jnp.stack(new_gla, axis=1)
    new_state_ret = jnp.stack(new_ret, axis=1)
    return (y_prompt, y_sample, new_state_hgrn, new_state_gla, new_state_ret)
```

```python
import numpy as np
from contextlib import ExitStack
import concourse.bass as bass
import concourse.mybir as mybir
from concourse.bass_utils import run_bass_kernel_spmd

F32 = mybir.dt.float32
BF16 = mybir.dt.bfloat16
AF = mybir.ActivationFunctionType
ALU = mybir.AluOpType

NCORES = 8
D = 1024
T = 1536
NB = 12
NSEG = 6
EPS = 1e-6
CL_H = -80.0
CL_G = -80.0 * 16.0

C_ID = 0
C_MF = 128
C_MB = 256
C_R = 384
C_SF = 512
C_SB = 1024
C_CF = 1536
C_CB = 1600
C_SEL = 1664
NCONST = 1664 + 768


class Tk:
    __slots__ = ("name", "w", "r", "x")

    def __init__(self, name="", x=False):
        self.name = name
        self.w = []
        self.r = []
        self.x = x


class EngW:
    def __init__(self, name):
        self.name = name
        self.cnt = 0
        self.seen = {}
        self.recs = []


class FW:
    def __init__(self, nc, stack, n_dma_sems=48):
        self.nc = nc
        self.sems = {}
        self.E = {}
        for nm in ("pe", "act", "dve", "pool", "sp"):
            self.sems[nm] = stack.enter_context(nc.semaphore("s_" + nm))
            self.E[nm] = EngW(nm)
        self.dma_free = []
        for i in range(n_dma_sems):
            self.sems["d%d" % i] = stack.enter_context(nc.semaphore("d%d" % i))
            self.dma_free.append("d%d" % i)
        self.dma_cnt = {}
        self.dma_named = {}
        self.n_ins = 0
        self.t_free = {nm: 0.0 for nm in self.E}
        self.t_done = {}
        self.last_end = 0.0

    def _dep_time(self, outs, ins, force=()):
        t = 0.0
        td = self.t_done
        for tk in ins:
            for tok in tk.w:
                t = max(t, td.get(tok, 0.0))
        for tk in outs:
            for tok in tk.w:
                t = max(t, td.get(tok, 0.0))
            for tok in tk.r:
                t = max(t, td.get(tok, 0.0))
        for tok in force:
            t = max(t, td.get(tok, 0.0))
        return t

    def _waits(self, e, outs, ins):
        deps = []
        for t in ins:
            deps.extend(t.w)
        for t in outs:
            deps.extend(t.w)
            deps.extend(t.r)
        best = {}
        for (k, v) in deps:
            if k == "pe" and e.name == "pe":
                continue
            if v <= e.seen.get(k, 0):
                continue
            if v > best.get(k, 0):
                best[k] = v
        for k, v in best.items():
            e.seen[k] = v
        return list(best.items())

    def op(self, eng, fn, outs=(), ins=(), force=(), cost=0.3):
        e = self.E[eng]
        outs = list(outs) + [t for t in ins if t.x]
        ins = [t for t in ins if not t.x]
        t0 = max(self.t_free[eng], self._dep_time(outs, ins, force) + 0.1)
        t1 = t0 + cost
        self.t_free[eng] = t1
        self.last_end = max(self.last_end, t0)
        waits = self._waits(e, outs, ins)
        for (k, v) in force:
            if v > e.seen.get(k, 0):
                e.seen[k] = v
                waits.append((k, v))
        e.cnt += 1
        tok = (e.name, e.cnt)
        self.t_done[tok] = t1
        e.recs.append((waits, fn, [(e.name, 1)]))
        for t in ins:
            t.r.append(tok)
        for t in outs:
            t.w = [tok]
            t.r = []
        self.n_ins += 1
        return tok

    def dma_sem(self, name):
        if name not in self.dma_named:
            k = self.dma_free.pop()
            self.dma_named[name] = k
            self.dma_cnt[k] = 0
        return self.dma_named[name]

    def dma(self, queue, semname, fn, outs=(), ins=(), fresh=True, nbytes=65536):
        e = self.E[queue]
        k = self.dma_sem(semname)
        t0 = max(self.t_free[queue], self._dep_time(outs, ins) + 0.1)
        self.t_free[queue] = t0 + (1.0 if queue == "pool" else 0.1)
        t1 = t0 + 2.0 + nbytes / 200e3
        waits = self._waits(e, outs, ins) if fresh else []
        self.dma_cnt[k] += 16
        tok = (k, self.dma_cnt[k])
        self.t_done[tok] = t1
        e.recs.append((waits, fn, [(k, 16)]))
        for t in ins:
            t.r.append(tok)
        for t in outs:
            if fresh:
                t.w = [tok]
                t.r = []
            else:
                t.w.append(tok)
        self.n_ins += 1
        return tok

    def fence(self, tks):
        best = {}
        for t in tks:
            for (k, v) in list(t.w) + list(t.r):
                if v > best.get(k, 0):
                    best[k] = v
        for q in self.E:
            self.wait_all(q, list(best.items()))

    def wait_all(self, queue, toks):
        e = self.E[queue]
        waits = []
        for (k, v) in toks:
            if v > e.seen.get(k, 0):
                e.seen[k] = v
                waits.append((k, v))
        if waits:
            e.recs.append((waits, None, []))

    def replay(self):
        nc = self.nc
        sems = self.sems

        def run(e):
            def body(eng):
                for (waits, fn, incs) in e.recs:
                    for (k, v) in waits:
                        eng.wait_ge(sems[k], v)
                    if fn is None:
                        continue
                    ins = fn(eng)
                    for (k, n) in incs:
                        ins.then_inc(sems[k], n)
            return body

        with nc.Block() as block:
            block.tensor(run(self.E["pe"]))
            block.scalar(run(self.E["act"]))
            block.vector(run(self.E["dve"]))
            block.gpsimd(run(self.E["pool"]))
            block.sync(run(self.E["sp"]))


class _Stop(Exception):
    pass


STOP_AT = None


def checkpoint(name):
    if STOP_AT == name:
        raise _Stop()


def build_program(debug=False):
    nc = bass.Bass("TRN2", target_bir_lowering=False)

    def din(name, shape):
        return nc.dram_tensor(name, list(shape), F32, kind="ExternalInput").ap()

    def dout(name, shape):
        return nc.dram_tensor(name, list(shape), F32, kind="ExternalOutput").ap()

    x_d = din("x", [T, D])
    cond_d = din("cond", [NSEG, D])
    flag_d = din("flag", [128, 1])
    s0h_d = din("s0h", [2, 4, 128, 128])
    s0g_d = din("s0g", [2, 4, 64, 128])
    s0r_d = din("s0r", [2, 8, 128, 128])
    cos_d = din("cosT", [128, T])
    sin_d = din("sinT", [128, T])
    consts_d = din("consts", [128, NCONST])
    m128_d = din("m128", [128, 256])
    norm_w_d = din("norm_w", [2, D])
    ada_w_d = din("ada_w", [2, D, 3 * D])
    ada_b_d = din("ada_b", [2, 3 * D])
    w_in_e_d = din("w_in_even", [1, D, 4128])
    hgrn_lb_d = din("hgrn_lb", [2, 512])
    gk_w_d = din("gla_gk_w", [1, 2, 16, 256])
    gk_b_d = din("gla_gk_b", [1, 2, 256])
    gn_e_d = din("gn_even", [1, D])
    w_out_e_d = din("w_out_even", [1, D, D])
    w_in_o_d = din("w_in_odd", [1, D, 4096])
    ret_dec_d = din("ret_decay", [1, 2, 8])
    gn_o_d = din("gn_odd", [1, D])
    w_out_o_d = din("w_out_odd", [1, D, D])
    fnw_d = din("final_norm_w", [D])

    y_d = dout("y", [T, D])
    sth_d = dout("st_h", [NSEG, 2, 4, 128, 128])
    stg_d = dout("st_g", [NSEG, 2, 4, 64, 128])
    str_d = dout("st_r", [NSEG, 2, 8, 128, 128])

    with ExitStack() as st:
        fw = FW(nc, st, n_dma_sems=48)

        def sb(name, shape, dt=F32):
            return st.enter_context(nc.sbuf_tensor(name, list(shape), dt))

        def ps(name, shape, dt=F32):
            return st.enter_context(nc.psum_tensor(name, list(shape), dt))

        x_sb = sb("x_sb", [128, NB, D])
        x_tk = [Tk("x%d" % b) for b in range(NB)]
        hT = sb("hT", [128, 8, T], BF16)
        hT_tk = [[Tk("hT%d_%d" % (b, kc)) for kc in range(8)] for b in range(NB)]
        ogT = sb("ogT", [128, 8, T], BF16)
        og_tk = [Tk("og%d" % h) for h in range(8)]
        Wb = [sb("W0", [128, 8, 800], BF16), sb("W1", [128, 8, 800], BF16)]
        W_tk = [[Tk("W0g%d" % g) for g in range(7)], [Tk("W1g%d" % g) for g in range(7)]]

        def wsel(wtk, c0, ncol):
            return wtk[c0 // 128:(c0 + ncol - 1) // 128 + 1]
        o_sb = sb("o_sb", [128, NB, 256])
        o_tk = [Tk("o%d" % b) for b in range(NB)]
        v_sbs = [sb("v_sb0", [128, NB, 256], BF16), sb("v_sb1", [128, NB, 256], BF16)]
        v_tks = [[Tk("v%d" % b) for b in range(NB)] for i in range(2)]
        AR_BYTES = 32 * 1024 + 512
        arena = sb("arena", [128, AR_BYTES // 4])
        consts = sb("consts_sb", [128, NCONST])
        consts_tk = Tk("consts")
        identb = sb("identb", [128, 128], BF16)
        Rb = sb("Rb", [128, 128], BF16)
        cb_tk = Tk("cb")
        colsA = sb("colsA", [128, 96])
        colsB = sb("colsB", [128, 32])
        cols_tk = Tk("cols")
        scondT = sb("scondT", [128, 48], BF16)
        scond_tk = Tk("scond")
        lbt = sb("lbt", [128, 16])
        lb_tk = Tk("lb")
        flag_sb = sb("flag_sb", [128, 1])
        flag_tk = Tk("flag")
        retd = sb("retd", [128, 48])
        retd_tk = Tk("retd")
        gkw_bf = sb("gkw_bf", [16, 2, 256], BF16)
        gkw_tk = Tk("gkw")
        modT = sb("modT", [128, 16, NSEG])
        mod_tk = Tk("modT")
        A_l = sb("A_l", [128, 8, NSEG])
        A_tk = Tk("A")
        grow = sb("grow", [NSEG, D])
        grow_tk = Tk("grow")
        tmp = arena[:, 0:1024]
        tmp_tk = Tk("tmp")
        xn_bf = arena[:, 1024:1536].bitcast(BF16)
        gbc = arena[:, 1536:2560]
        gbc_tk = Tk("gbc")
        fnw_bc = arena[:, 2560:3584]
        fnw_tk = Tk("fnw")
        arena_tks = [tmp_tk, gbc_tk, fnw_tk]
        stageA = tmp[:, 0:128]
        stageB = tmp[0:32, 128:256]
        xn_tk = Tk("xn")
        arena_tks.append(xn_tk)
        stat = sb("stat", [128, 64])
        stat_tk = Tk("stat")
        S32 = [[sb("S32_%d_%d" % (d, i), [128, 128]) for i in range(4)] for d in range(2)]
        S32_tk = [[Tk("S32") for i in range(4)] for d in range(2)]
        Sbf = [[sb("Sbf_%d_%d" % (d, i), [128, 128], BF16) for i in range(4)] for d in range(2)]
        Sbf_tk = [[Tk("Sbf") for i in range(4)] for d in range(2)]
        PT = [[[sb("PT_%d_%d_%d" % (d, i, j), [128, 128], BF16) for j in range(2)] for i in range(2)] for d in range(2)]
        PT_tk = [[[Tk("PT") for j in range(2)] for i in range(2)] for d in range(2)]
        ktok = [sb("ktok_%d" % i, [128, 128], BF16) for i in range(2)]
        ktok_tk = [Tk("ktok") for i in range(2)]
        Gq = [[sb("Gq_%d_%d" % (d, i), [128, 8]) for i in range(2)] for d in range(2)]
        sg_q = sb("sg_q", [128, 512], BF16)
        sgq_tk = Tk("sgq")

        print("SBUF bytes remaining/partition:", nc.sbuf_bytes_remaining)

        PA = ps("PA", [128, 512])
        PB = ps("PB", [128, 512])
        PAB = [PA, PB]
        PAB_tk = [Tk("PA", True), Tk("PB", True)]
        Y = ps("Y", [128, 1024])
        Yh_tk = [Tk("Y0", True), Tk("Y1", True)]
        O = [ps("O0", [128, 512]), ps("O1", [128, 512])]
        O_tk = [Tk("O0", True), Tk("O1", True)]
        OB = [[(O[0][:, :], O_tk[0]), (O[1][:, :], O_tk[1])],
              [(Y[:, 0:512], Yh_tk[0]), (Y[:, 512:1024], Yh_tk[1])]]
        M = [ps("M0", [128, 512]), ps("M1", [128, 512])]
        M_tk = [Tk("M0", True), Tk("M1", True)]
        pab_i = [0]

        def next_pab():
            i = pab_i[0]
            pab_i[0] ^= 1
            return PAB[i], PAB_tk[i]

        def fsz(ap):
            n = 1
            for d_ in ap.shape[1:]:
                n *= int(d_)
            return n

        def ecost(eng, ap):
            f = fsz(ap)
            if eng == "act":
                return 0.22 + f / 1200.0
            if eng == "pool":
                return 0.15 + f * 2.6e-3
            return 0.2 + f / 960.0

        def ACT(out, in_, func, outs, ins, **kw):
            fw.op("act", lambda e: e.activation(out=out, in_=in_, func=func, **kw), outs=outs, ins=ins, cost=ecost("act", out))

        def TT(eng, out, in0, in1, op, outs, ins):
            fw.op(eng, lambda e: e.tensor_tensor(out=out, in0=in0, in1=in1, op=op), outs=outs, ins=ins, cost=ecost(eng, out))

        def TS(eng, out, in0, s1, s2, op0, op1, outs, ins):
            if s2 is None:
                fw.op(eng, lambda e: e.tensor_scalar(out=out, in0=in0, scalar1=s1, scalar2=None, op0=op0), outs=outs, ins=ins, cost=ecost(eng, out))
            else:
                fw.op(eng, lambda e: e.tensor_scalar(out=out, in0=in0, scalar1=s1, scalar2=s2, op0=op0, op1=op1), outs=outs, ins=ins, cost=ecost(eng, out))

        def STT(out, in0, scalar, in1, op0, op1, outs, ins):
            fw.op("dve", lambda e: e.scalar_tensor_tensor(out=out, in0=in0, scalar=scalar, in1=in1, op0=op0, op1=op1), outs=outs, ins=ins,
                  cost=ecost("dve", out))

        def PE(mms, outs, ins, force=()):
            def fn(e):
                r = None
                for (o, l, rh, s1, s2) in mms:
                    r = e.matmul(o, lhsT=l, rhs=rh, start=s1, stop=s2)
                return r
            c = 0.15
            for (o, l, rh, s1, s2) in mms:
                c += max(0.06, fsz(rh) / 2400.0 + 0.01)
            return fw.op("pe", fn, outs=outs, ins=ins, force=force, cost=c)

        def nbytes(ap):
            n = 4
            for d_ in ap.shape:
                n *= int(d_)
            return n

        def LOAD(queue, sem, out, in_, outs, fresh=True):
            fw.dma(queue, sem, lambda e: e.dma_start(out=out, in_=in_), outs=outs, fresh=fresh, nbytes=nbytes(out))

        def STORE(queue, sem, out, in_, ins):
            fw.dma(queue, sem, lambda e: e.dma_start(out=out, in_=in_), ins=ins, nbytes=nbytes(in_))

        def w_view(w2d):
            return w2d.rearrange("(kc p) n -> p kc n", p=128)

        def load_cols(slot, off, src2d, c0, ncol, fresh):
            src = w_view(src2d)[:, :, c0:c0 + ncol]
            LOAD("pool", "W%d_%d" % (slot, off // 128), Wb[slot][:, :, off:off + ncol], src, wsel(W_tk[slot], off, ncol))

        def cols_l0(hg):
            if hg < 4:
                h = hg
                return [(128 * j, c0, 128) for j, c0 in enumerate([128 * h, 512 + 128 * h, 1024 + 128 * h, 1536 + 128 * h, 2048 + 128 * h])]
            p = hg - 4
            return [(0, 2560 + 128 * p, 128), (128, 2816 + 128 * p, 128), (256, 3072 + 256 * p, 256), (512, 3584 + 256 * p, 256), (768, 4096, 32)]

        def fin_groups_l0(hg):
            return {4} if hg < 4 else {4, 5}

        def load_hg_l0(slot, hg, phase="all", busy=()):
            for (off, c0, ncol) in cols_l0(hg):
                groups = set(range(off // 128, (off + ncol - 1) // 128 + 1))
                hit = bool(groups & set(busy))
                if phase == "all" or (phase == "early" and not hit) or (phase == "late" and hit):
                    load_cols(slot, off, w_in_e_d[0], c0, ncol, True)

        def load_hg_l1(slot, h, phase="all"):
            w = w_in_o_d[0]
            for j in range(4):
                late = (j == 3)
                if phase == "all" or (phase == "early" and not late) or (phase == "late" and late):
                    load_cols(slot, 128 * j, w, 1024 * j + 128 * h, 128, True)

        LOAD("sp", "c0", consts[:, :], consts_d[:, :], [consts_tk])
        LOAD("sp", "c1", flag_sb[:, :], flag_d[:, :], [flag_tk])
        LOAD("sp", "c2", retd[:, 0:16], ret_dec_d[0].rearrange("a b -> (a b)").partition_broadcast(128), [retd_tk])
        stA_tk = tmp_tk
        LOAD("sp", "c4", stageA[0:48, :], cond_d.rearrange("s (kc j) -> (s kc) j", j=128), [stA_tk])
        LOAD("sp", "c4", stageA[48:64, :], norm_w_d.rearrange("l (kc j) -> (l kc) j", j=128), [stA_tk], fresh=False)
        LOAD("sp", "c4", stageA[64:72, :], gn_e_d.rearrange("l (kc j) -> (l kc) j", j=128), [stA_tk], fresh=False)
        LOAD("sp", "c4", stageA[72:80, :], gn_o_d.rearrange("l (kc j) -> (l kc) j", j=128), [stA_tk], fresh=False)
        LOAD("sp", "c4", stageA[80:88, :], hgrn_lb_d.rearrange("l (kc j) -> (l kc) j", j=128), [stA_tk], fresh=False)
        LOAD("sp", "c4", stageA[88:92, :], gk_b_d[0].rearrange("l (kc j) -> (l kc) j", j=128), [stA_tk], fresh=False)
        LOAD("pool", "c5", gkw_bf[:, :, :], gk_w_d[0].rearrange("d r n -> r d n"), [gkw_tk])
        for blk in range(NB):
            LOAD("sp", "x%d" % blk, x_sb[:, blk, :], x_d[blk * 128:(blk + 1) * 128, :], [x_tk[blk]])

        identf = consts[:, C_ID:C_ID + 128]
        TS("dve", identb[:, :], identf, 1.0, None, ALU.mult, None, [cb_tk], [consts_tk])
        TS("dve", Rb[:, :], consts[:, C_R:C_R + 128], 1.0, None, ALU.mult, None, [cb_tk], [consts_tk])

        pt, pt_tk = next_pab()
        PE([(pt[:, 0:92], stageA[0:92, :], consts[0:92, C_ID:C_ID + 92], True, True)], [pt_tk], [stA_tk, consts_tk])
        fw.op("act", lambda e, pt=pt: e.copy(out=colsA[:, 0:92], in_=pt[:, 0:92]), outs=[cols_tk], ins=[pt_tk])
        stB_tk = tmp_tk
        LOAD("sp", "c6", stageB[0:16, :], ada_b_d[0, 0:2048].rearrange("(kc j) -> kc j", j=128), [stB_tk])
        LOAD("sp", "c6", stageB[16:32, :], ada_b_d[1, 0:2048].rearrange("(kc j) -> kc j", j=128), [stB_tk], fresh=False)
        pt, pt_tk = next_pab()
        PE([(pt[:, 0:32], stageB[0:32, :], consts[0:32, C_ID:C_ID + 32], True, True)], [pt_tk], [stB_tk, consts_tk])
        fw.op("act", lambda e, pt=pt: e.copy(out=colsB[:, 0:32], in_=pt[:, 0:32]), outs=[cols_tk], ins=[pt_tk])
        ACT(scondT[:, :], colsA[:, 0:48], AF.Silu, [scond_tk], [cols_tk])
        TT("dve", lbt[:, 0:4], colsA[:, 80:84], colsA[:, 84:88], ALU.subtract, [lb_tk], [cols_tk])
        ACT(lbt[:, 0:4], lbt[:, 0:4], AF.Sigmoid, [lb_tk], [lb_tk])
        TS("dve", lbt[:, 4:8], lbt[:, 0:4], -0.5, 0.5, ALU.mult, ALU.add, [lb_tk], [lb_tk])
        TS("dve", lbt[:, 8:12], lbt[:, 0:4], 0.5, 0.5, ALU.mult, ALU.add, [lb_tk], [lb_tk])
        TS("dve", lbt[:, 12:16], colsA[:, 88:92], 0.5, None, ALU.mult, None, [lb_tk], [lb_tk, cols_tk])
        ACT(retd[:, 16:32], retd[:, 0:16], AF.Sigmoid, [retd_tk], [retd_tk])

        scond_k = scondT[:, :].rearrange("p (s k) -> p k s", k=8)

        state = {"s32": 0, "sbf": 0, "u": 0}

        Xb = [arena[:, 3584:5632].bitcast(BF16).rearrange("p (k n) -> p k n", n=512),
              arena[:, 5632:7680].bitcast(BF16).rearrange("p (k n) -> p k n", n=512)]
        X_tk = [Tk("X0"), Tk("X1")]
        arena_tks += X_tk

        def ada_buf(l, cg):
            if l == 0:
                i = cg % 4
                if i < 2:
                    return Xb[i], [X_tk[i]], "ada%d" % i
                return Wb[i - 2][:, :, 0:512], W_tk[i - 2][0:4], "ada%d" % i
            if cg in (0, 1):
                return Xb[cg], [X_tk[cg]], "ada%d" % cg
            if cg == 5:
                return Xb[0], [X_tk[0]], "ada0"
            j = cg - 2
            return hT[:, :, j * 512:(j + 1) * 512], [t for bb in range(4 * j, 4 * j + 4) for t in hT_tk[bb]], "ada%d" % cg

        def ada_load(l, cg):
            buf, tks, sem = ada_buf(l, cg)
            src = w_view(ada_w_d[l])[:, :, cg * 512:(cg + 1) * 512]
            LOAD("pool", sem, buf, src, tks)

        def adaln(l, preloaded=0):
            ptf, ptf_tk = next_pab()
            LOAD("sp", "c7", grow[:, :], ada_b_d[l, 2048:3072].partition_broadcast(NSEG), [grow_tk])
            nbuf = 4 if l == 0 else 5
            for cg in range(preloaded, nbuf):
                ada_load(l, cg)
            for cg in range(6):
                W, wtks, _ = ada_buf(l, cg)
                if cg < 4:
                    mms = []
                    for j in range(4):
                        fb = cg * 4 + j
                        for kc in range(8):
                            mms.append((ptf[:, fb * 6:fb * 6 + 6], W[:, kc, j * 128:(j + 1) * 128], scond_k[:, kc, :], kc == 0, kc == 7))
                    PE(mms, [ptf_tk], wtks + [scond_tk])
                else:
                    pr = Y
                    half = cg - 4
                    mms = [(pr[0:NSEG, half * 512:(half + 1) * 512], scond_k[:, kc, :], W[:, kc, 0:512], kc == 0, kc == 7) for kc in range(8)]
                    PE(mms, [Yh_tk[half]], wtks + [scond_tk])
                    TT("dve", grow[:, half * 512:(half + 1) * 512], pr[0:NSEG, half * 512:(half + 1) * 512],
                       grow[:, half * 512:(half + 1) * 512], ALU.add, [grow_tk], [Yh_tk[half], grow_tk])
                if cg + nbuf < 6:
                    ada_load(l, cg + nbuf)
            ptf3 = ptf[:, 0:96].rearrange("p (f s) -> p f s", s=NSEG)
            TT("dve", modT[:, :, :], ptf3, colsB[:, 16 * l:16 * l + 16].unsqueeze(2).to_broadcast([128, 16, NSEG]), ALU.add,
               [mod_tk], [ptf_tk, cols_tk])
            TS("dve", A_l[:, :, :], modT[:, 8:16, :], 1.0, None, ALU.add, None, [A_tk], [mod_tk])
            TT("dve", A_l[:, :, :], A_l[:, :, :], colsA[:, 48 + 8 * l:56 + 8 * l].unsqueeze(2).to_broadcast([128, 8, NSEG]), ALU.mult,
               [A_tk], [A_tk, cols_tk])

        st_tks = [Tk("st%d" % b) for b in range(NB)]
        xn2 = [xn_bf, arena[:, 7680:8192].bitcast(BF16)]
        xn2_tk = [xn_tk, Tk("xn1")]
        arena_tks.append(xn2_tk[1])

        def norm_mod(l):
            def stA(b):
                i = b % 2
                ACT(tmp[:, :], x_sb[:, b, :], AF.Square, [tmp_tk, st_tks[b]], [x_tk[b]], accum_out=stat[:, b:b + 1])
                ACT(stat[:, 16 + b:17 + b], stat[:, b:b + 1], AF.Sqrt, [st_tks[b]], [st_tks[b], eps_tk], scale=1.0 / D, bias=eps_ap)
                fw.op("dve", lambda e, b=b: e.reciprocal(out=stat[:, 32 + b:33 + b], in_=stat[:, 16 + b:17 + b]), outs=[st_tks[b]], ins=[st_tks[b]])
                TS("dve", xn2[i][:, :], x_sb[:, b, :], stat[:, 32 + b:33 + b], None, ALU.mult, None, [xn2_tk[i]], [x_tk[b], st_tks[b]])
                banks = [(PA, PAB_tk[0]), (PB, PAB_tk[1])] if i == 0 else [(O[0], O_tk[0]), (O[1], O_tk[1])]
                for hf in range(2):
                    pt, pt_tk = banks[hf]
                    mms = [(pt[:, j * 128:(j + 1) * 128], xn2[i][:, (hf * 4 + j) * 128:(hf * 4 + j + 1) * 128], identb[:, :], True, True) for j in range(4)]
                    PE(mms, [pt_tk], [xn2_tk[i], cb_tk])

            def stB(b):
                i = b % 2
                seg = b // 2
                banks = [(PA, PAB_tk[0]), (PB, PAB_tk[1])] if i == 0 else [(O[0], O_tk[0]), (O[1], O_tk[1])]
                for hf in range(2):
                    pt, pt_tk = banks[hf]
                    for j in range(4):
                        kc = hf * 4 + j
                        if hf == 0:
                            ACT(hT[:, kc, b * 128:(b + 1) * 128], pt[:, j * 128:(j + 1) * 128], AF.Identity, [hT_tk[b][kc]], [pt_tk, A_tk, mod_tk],
                                scale=A_l[:, kc, seg:seg + 1], bias=modT[:, kc, seg:seg + 1])
                        else:
                            TS("dve", hT[:, kc, b * 128:(b + 1) * 128], pt[:, j * 128:(j + 1) * 128], A_l[:, kc, seg:seg + 1], modT[:, kc, seg:seg + 1],
                               ALU.mult, ALU.add, [hT_tk[b][kc]], [pt_tk, A_tk, mod_tk])

            stA(0)
            for b in range(NB):
                if b + 1 < NB:
                    stA(b + 1)
                stB(b)

        eps_t = sb("eps_t", [128, 1])
        eps_tk = Tk("eps")
        fw.op("dve", lambda e: e.memset(eps_t[:, :], EPS), outs=[eps_tk])
        eps_ap = eps_t[:, 0:1]
        nhalf = sb("nhalf", [128, 16])
        nh_tk = Tk("nh")
        fw.op("dve", lambda e: e.memset(nhalf[:, :], -0.5), outs=[nh_tk])
        cst_t = sb("cst_t", [128, 2])
        cst_tk = Tk("cst")
        fw.op("dve", lambda e: e.memset(cst_t[:, 0:1], 1.0), outs=[cst_tk])
        fw.op("dve", lambda e: e.memset(cst_t[:, 1:2], 0.5), outs=[cst_tk])
        one_ap = cst_t[:, 0:1]
        half_ap = cst_t[:, 1:2]

        def proj_fm(W, wtk, c0, ncol, quad, pt, pt_tk, tok0=None, ntok=512):
            if tok0 is None:
                tok0 = quad * 512
            mms = [(pt[0:ncol, 0:ntok], W[:, kc, c0:c0 + ncol], hT[:, kc, tok0:tok0 + ntok], kc == 0, kc == 7) for kc in range(8)]
            PE(mms, [pt_tk], wsel(wtk, c0, ncol) + [t for bb in range(tok0 // 128, (tok0 + ntok) // 128) for t in hT_tk[bb]])

        def proj_v(W, wtk, c0, nv, b, vb):
            v_sb, v_tk = v_sbs[vb], v_tks[vb]
            pt, pt_tk = next_pab()
            mms = [(pt[:, 0:nv], hT[:, kc, b * 128:(b + 1) * 128], W[:, kc, c0:c0 + nv], kc == 0, kc == 7) for kc in range(8)]
            PE(mms, [pt_tk], wsel(wtk, c0, nv) + hT_tk[b])
            fw.op("act", lambda e: e.copy(out=v_sb[:, b, 0:nv], in_=pt[:, 0:nv]), outs=[v_tk[b]], ins=[pt_tk])

        maskT64 = [consts[:, C_MF:C_MF + 128], consts[:, C_MB:C_MB + 128]]
        state = [{"s32": 0, "sbf": 0}, {"s32": 0, "sbf": 0}]
        NSBF = 4

        def seg_start(d, seg, s0_src):
            stt = state[d]
            cur = stt["s32"] % 4
            nxt = (stt["s32"] + 1) % 4
            nb = (stt["sbf"] + 1) % NSBF
            first_seg = 0 if d == 0 else 3
            chained = (1 <= seg <= 3) if d == 0 else (0 <= seg <= 2)
            if seg == first_seg:
                LOAD("sp", "s32_%d_%d" % (d, nxt), S32[d][nxt][:, :], s0_src, [S32_tk[d][nxt]])
                fw.op("act", lambda e: e.copy(out=Sbf[d][nb][:, :], in_=S32[d][nxt][:, :]), outs=[Sbf_tk[d][nb]], ins=[S32_tk[d][nxt]])
            elif chained:
                TS("dve", S32[d][nxt][:, :], S32[d][cur][:, :], flag_sb[:, 0:1], None, ALU.mult, None, [S32_tk[d][nxt]], [S32_tk[d][cur], flag_tk])
                fw.op("act", lambda e: e.copy(out=Sbf[d][nb][:, :], in_=S32[d][nxt][:, :]), outs=[Sbf_tk[d][nb]], ins=[S32_tk[d][nxt]])
            else:
                fw.op("pool", lambda e: e.memset(S32[d][nxt][:, :], 0.0), outs=[S32_tk[d][nxt]])
                fw.op("pool", lambda e: e.memset(Sbf[d][nb][:, :], 0.0), outs=[Sbf_tk[d][nb]])
            stt["s32"] += 1
            stt["sbf"] += 1

        class Blk:
            def __init__(self, b, d, subs, qt, kt, ktr, qk_tks, lo, G_of, g_tks, st_dst, s0_src, first_visit, vb, ktok_scale=None,
                         nch=2, maskT=None, mask_tks=None):
                self.__dict__.update(locals())
                self.v_sb = v_sbs[vb]
                self.v_tk = v_tks[vb]
                self.nsub = len(subs)
                self.pi = b % 2
                self.maskT = maskT if maskT is not None else maskT64
                self.mask_tks = mask_tks if mask_tks is not None else [consts_tk]
                self.CH = 128 // nch
                if nch == 2:
                    self.order = [0, 1] if d == 0 else [1, 0]
                else:
                    self.order = [0]
                if self.nsub == 1 and nch == 2:
                    self.kvcol = [128, 384]
                else:
                    self.kvcol = [384, 384]
                self.sbf_idx = []

            def A(self):
                d, b, lo = self.d, self.b, self.lo
                Mb, Mtk = M[d], M_tk[d]
                ptok = None
                for j, (p0, dk, vc) in enumerate(self.subs):
                    ptok = PE([(Mb[:, j * 128:(j + 1) * 128], self.kt[p0:p0 + dk, lo:lo + 128], self.qt[p0:p0 + dk, lo:lo + 128], True, True)],
                              [Mtk], self.qk_tks, force=([ptok] if ptok else []))
                if self.nsub == 1:
                    pass
                PE([(Mb[:, 256:384], self.ktr[:, lo:lo + 128], identb[:, :], True, True)], [Mtk], self.qk_tks + [cb_tk])
                for j in range(self.nsub):
                    TT("dve", PT[d][self.pi][j][:, :], Mb[:, j * 128:(j + 1) * 128], self.maskT[d], ALU.mult, [PT_tk[d][self.pi][j]], [Mtk] + self.mask_tks)
                if self.ktok_scale is None:
                    fw.op("act", lambda e: e.copy(out=ktok[d][:, :], in_=Mb[:, 256:384]), outs=[ktok_tk[d]], ins=[Mtk])
                else:
                    ACT(ktok[d][:, :], Mb[:, 256:384], AF.Identity, [ktok_tk[d]], [Mtk] + self.g_tks, scale=self.ktok_scale)

            def kv(self, ci, force=()):
                d, b = self.d, self.b
                Mb, Mtk = M[d], M_tk[d]
                r0 = self.order[ci] * self.CH
                kc = self.kvcol[ci]
                mms = []
                for j, (p0, dk, vc) in enumerate(self.subs):
                    mms.append((Mb[p0:p0 + dk, kc:kc + 128], ktok[d][r0:r0 + self.CH, p0:p0 + dk], self.v_sb[r0:r0 + self.CH, b, vc:vc + 128], True, True))
                return PE(mms, [Mtk], [ktok_tk[d], self.v_tk[b]], force=force)

            def rec(self, ci):
                d = self.d
                stt = state[d]
                Mb, Mtk = M[d], M_tk[d]
                r0 = self.order[ci] * self.CH
                kc = self.kvcol[ci]
                cur = stt["s32"] % 4
                nxt = (stt["s32"] + 1) % 4
                if ci == 0:
                    self.sbf_idx.append(stt["sbf"] % NSBF)
                G = self.G_of(r0)
                STT(S32[d][nxt][:, :], S32[d][cur][:, :], G, Mb[:, kc:kc + 128], ALU.mult, ALU.add, [S32_tk[d][nxt]], [S32_tk[d][cur], Mtk] + self.g_tks)
                nbf = (stt["sbf"] + 1) % NSBF
                fw.op("act", lambda e, nbf=nbf, nxt=nxt: e.copy(out=Sbf[d][nbf][:, :], in_=S32[d][nxt][:, :]), outs=[Sbf_tk[d][nbf]], ins=[S32_tk[d][nxt]])
                self.sbf_idx.append(nbf)
                stt["s32"] += 1
                stt["sbf"] += 1

            def BC(self):
                d, b = self.d, self.b
                seg = b // 2
                at_start = (b % 2 == 0) if d == 0 else (b % 2 == 1)
                if at_start:
                    seg_start(d, seg, self.s0_src)
                n = len(self.order)
                if self.kvcol[0] != self.kvcol[1] or n == 1:
                    ptok = None
                    for ci in range(n):
                        ptok = self.kv(ci, force=([ptok] if ptok else []))
                    yield
                    for ci in range(n):
                        self.rec(ci)
                else:
                    for ci in range(n):
                        self.kv(ci)
                        yield
                        self.rec(ci)
                if not at_start:
                    cur = state[d]["s32"] % 4
                    STORE("sp", "s32_%d_%d" % (d, cur), self.st_dst(seg), S32[d][cur][:, :], [S32_tk[d][cur]])

            def D(self):
                d, b, lo = self.d, self.b, self.lo
                if self.nsub == 1:
                    obk = [OB[d][b % 2]]
                else:
                    obk = [OB[d][0], OB[d][1]]
                mms = []
                ins = list(self.qk_tks) + [self.v_tk[b]]
                for j, (p0, dk, vc) in enumerate(self.subs):
                    Ob = obk[j][0]
                    mms.append((Ob[:, 0:128], PT[d][self.pi][j][:, :], self.v_sb[:, b, vc:vc + 128], True, False))
                    ins.append(PT_tk[d][self.pi][j])
                for ci in range(len(self.order)):
                    r0 = self.order[ci] * self.CH
                    sbi = self.sbf_idx[ci]
                    ins.append(Sbf_tk[d][sbi])
                    for j, (p0, dk, vc) in enumerate(self.subs):
                        Ob = obk[j][0]
                        mms.append((Ob[r0:r0 + self.CH, 0:128], self.qt[p0:p0 + dk, lo + r0:lo + r0 + self.CH], Sbf[d][sbi][p0:p0 + dk, :], False, True))
                mms_sorted = [m for m in mms if m[3]] + [m for m in mms if not m[3]]
                PE(mms_sorted, [ob[1] for ob in obk], ins)
                for j in range(self.nsub):
                    Obj, Obj_tk = obk[j]
                    if b not in self.first_visit:
                        fw.op("act", lambda e, Obj=Obj, j=j: e.copy(out=o_sb[:, b, j * 128:(j + 1) * 128], in_=Obj[:, 0:128]), outs=[o_tk[b]], ins=[Obj_tk])
                    else:
                        TT("dve", o_sb[:, b, j * 128:(j + 1) * 128], Obj[:, 0:128], o_sb[:, b, j * 128:(j + 1) * 128], ALU.add, [o_tk[b]], [Obj_tk, o_tk[b]])
                self.first_visit.add(b)

        def scan_chain_gen(blks):
            prev = None
            for blk in blks:
                blk.A()
                yield
                if prev is not None:
                    prev.D()
                    yield
                yield from blk.BC()
                yield
                prev = blk
            prev.D()
            yield

        def run_interleaved(gens, weights=None):
            gens = list(gens)
            weights = list(weights) if weights else [1] * len(gens)
            alive = [True] * len(gens)
            while any(alive):
                for i, g in enumerate(gens):
                    for _ in range(weights[i]):
                        if not alive[i]:
                            break
                        try:
                            next(g)
                        except StopIteration:
                            alive[i] = False

        def chain(gs):
            for g in gs:
                yield from g

        ssq_tk = [Tk("ssq%d" % b) for b in range(NB)]

        def finalize(W, wtk, subs_fin, on_bf, on_tks):
            for (oc, gcol, ch, gncol) in subs_fin:
                TT("dve", on_bf[:, :, :], o_sb[:, :, oc:oc + 128], o_sb[:, :, oc:oc + 128], ALU.mult, on_tks, o_tk)
                yield
                fw.op("dve", lambda e: e.tensor_reduce(out=stat[:, 0:NB], in_=on_bf[:, :, :], axis=mybir.AxisListType.X, op=ALU.add),
                      outs=[stat_tk], ins=on_tks, cost=1.7)
                TS("dve", stat[:, 16:16 + NB], stat[:, 0:NB], 1.0 / 128, EPS, ALU.mult, ALU.add, [stat_tk], [stat_tk])
                yield
                TT("pool", stat[:, 32:32 + NB], stat[:, 16:16 + NB], nhalf[:, 0:NB], ALU.pow, [stat_tk], [stat_tk, nh_tk])
                yield
                TT("dve", on_bf[:, :, :], o_sb[:, :, oc:oc + 128], stat[:, 32:32 + NB].unsqueeze(2).to_broadcast([128, NB, 128]), ALU.mult,
                   on_tks, o_tk + [stat_tk])
                yield
                for quad in range(3):
                    pg, pg_tk = next_pab()
                    proj_fm(W, wtk, gcol, 128, quad, pg, pg_tk)
                    ACT(sg_q[:, :], pg[:, 0:512], AF.Silu, [sgq_tk], [pg_tk])
                    yield
                    pt, pt_tk = next_pab()
                    mms = [(pt[:, j * 128:(j + 1) * 128], on_bf[:, quad * 4 + j, :], identb[:, :], True, True) for j in range(4)]
                    PE(mms, [pt_tk], on_tks + [cb_tk])
                    STT(ogT[:, ch, quad * 512:(quad + 1) * 512], pt[:, 0:512], colsA[:, gncol:gncol + 1], sg_q[:, :], ALU.mult, ALU.mult,
                        [og_tk[ch]], [pt_tk, cols_tk, sgq_tk])
                    yield

        def gen_V(W, wtk, c0, nv, vb):
            for b in range(NB):
                proj_v(W, wtk, c0, nv, b, vb)
                yield

        def out_proj(l, w_out2d, last):
            fw.fence(arena_tks)
            if not last:
                for cg_ in range(5):
                    ada_load(1, cg_)
            if last:
                LOAD("sp", "c3", fnw_bc[:, :], fnw_d.partition_broadcast(128), [fnw_tk])
            for b in range(NB):
                seg = b // 2
                if b % 2 == 0:
                    for hf in range(2):
                        PE([(M[hf][:, 0:512], consts[0:NSEG, C_SEL + seg * 128:C_SEL + (seg + 1) * 128], grow[:, hf * 512:(hf + 1) * 512], True, True)],
                           [M_tk[hf]], [consts_tk, grow_tk])
                        fw.op("act", lambda e, hf=hf: e.copy(out=gbc[:, hf * 512:(hf + 1) * 512], in_=M[hf][:, 0:512]), outs=[gbc_tk], ins=[M_tk[hf]], cost=0.65)
                banks = OB[1] if b % 2 == 0 else OB[0]
                for hf in range(2):
                    bk, bk_tk = banks[hf]
                    mms = [(bk[:, 0:512], ogT[:, ch, b * 128:(b + 1) * 128], Wb[hf][:, ch, 0:512], ch == 0, ch == 7) for ch in range(8)]
                    PE(mms, [bk_tk], og_tk + W_tk[hf][0:4])
                    TT("dve", tmp[:, hf * 512:(hf + 1) * 512], bk[:, 0:512], gbc[:, hf * 512:(hf + 1) * 512], ALU.mult, [tmp_tk], [bk_tk, gbc_tk])
                TT("dve", x_sb[:, b, :], tmp[:, :], x_sb[:, b, :], ALU.add, [x_tk[b]], [tmp_tk, x_tk[b]])
                if last:
                    if b >= 1:
                        fin_norm(b - 1)
            if last:
                fin_norm(NB - 1)

        def fin_norm(b):
            ACT(xn_bf[:, :], x_sb[:, b, :], AF.Square, [xn_tk, st_tks[b]], [x_tk[b]], accum_out=stat[:, b:b + 1])
            ACT(stat[:, 16 + b:17 + b], stat[:, b:b + 1], AF.Sqrt, [st_tks[b]], [st_tks[b], eps_tk], scale=1.0 / D, bias=eps_ap)
            fw.op("dve", lambda e, b=b: e.reciprocal(out=stat[:, 32 + b:33 + b], in_=stat[:, 16 + b:17 + b]), outs=[st_tks[b]], ins=[st_tks[b]])
            STT(x_sb[:, b, :], x_sb[:, b, :], stat[:, 32 + b:33 + b], fnw_bc[:, :], ALU.mult, ALU.mult, [x_tk[b]], [x_tk[b], st_tks[b], fnw_tk])
            STORE("sp", "yout", y_d[b * 128:(b + 1) * 128, :], x_sb[:, b, :], [x_tk[b]])

        def ar(off_bytes, ncols, dt=F32):
            esz = 4 if dt == F32 else 2
            a_ = arena[:, off_bytes // 4:(off_bytes + ncols * esz) // 4]
            return a_ if dt == F32 else a_.bitcast(BF16)

        KB = 1024
        NH = 512
        scr = []
        scr_tk = []
        for d_ in range(2):
            base = d_ * 8 * KB
            scr.append(dict(sig=ar(base, NH), kk=ar(base + 2 * KB, NH), Ep=ar(base + 4 * KB, NH), Em=ar(base, NH),
                            q=ar(base + 6 * KB, NH), gl=ar(base + 4 * KB, NH), low=ar(base, NH, BF16)))
            tsig, tkk, tEp, tq = Tk("sig"), Tk("kk"), Tk("Ep"), Tk("q")
            scr_tk.append(dict(sig=tsig, kk=tkk, Ep=tEp, Em=tsig, q=tq, gl=tEp))
        Pq = [[[ar(16 * KB + ((d * 2 + i) * 3 + k) * KB, 512, BF16) for k in range(3)] for i in range(2)] for d in range(2)]
        Pq_tk = [[dict(qt=[Tk("qt"), Tk("qt")], kt=[Tk("kt"), Tk("kt")], kd=[Tk("kd") for c in range(8)]) for i in range(2)] for d in range(2)]
        Gq_tk = [[[Tk("Gq"), Tk("Gq")] for i in range(2)] for d in range(2)]
        on_bf0 = ar(28 * KB, T, BF16).rearrange("p (b v) -> p b v", v=128)
        on_tk0 = Tk("on0")
        arena_tks += [on_tk0]
        for d_ in range(2):
            arena_tks += list(scr_tk[d_].values())
            for i_ in range(2):
                arena_tks += Pq_tk[d_][i_]["qt"] + Pq_tk[d_][i_]["kt"] + Pq_tk[d_][i_]["kd"]
        rmask = [consts[:, C_SF:C_SF + 512], consts[:, C_SB:C_SB + 512]]
        PMIN = 1.8e-35

        try:
            checkpoint('setup')
            adaln(0)
            checkpoint('adaln0')
            load_hg_l0(0, 0)
            load_hg_l0(1, 1)
            norm_mod(0)
            checkpoint('norm0')

            def gates_l0(hg, W, wtk, d, quad, half, buf):
                is_h = hg < 4
                pr = hg - 4
                tok0 = quad * 512 + half * NH
                h0 = half * NH
                T_ = scr[d]
                K_ = scr_tk[d]
                sig_t, kk_t, Ep_t, Em_t, q_t, gl_t, low_t = T_["sig"], T_["kk"], T_["Ep"], T_["Em"], T_["q"], T_["gl"], T_["low"]
                qt_t, kt_t, kd_t = Pq[d][buf]
                ptk = Pq_tk[d][buf]
                if is_h:
                    pf, pf_tk = next_pab()
                    proj_fm(W, wtk, 256 + 128 * d, 128, quad, pf, pf_tk, tok0, NH)
                    ACT(sig_t, pf[:, 0:NH], AF.Tanh, [K_["sig"]], [pf_tk], scale=0.5)
                    yield
                    TS("dve", sig_t, sig_t, lbt[:, 4 + hg:5 + hg], lbt[:, 8 + hg:9 + hg], ALU.mult, ALU.add, [K_["sig"]], [K_["sig"], lb_tk])
                    ACT(kk_t, sig_t, AF.Identity, [K_["kk"]], [K_["sig"], cst_tk], scale=-1.0, bias=one_ap)
                    yield
                    pq, pq_tk = next_pab()
                    proj_fm(W, wtk, 0, 128, quad, pq, pq_tk, tok0, NH)
                    ACT(q_t, pq[:, 0:NH], AF.Silu, [K_["q"]], [pq_tk])
                else:
                    pl, pl_tk = next_pab()
                    proj_fm(W, wtk, 768 + 16 * d, 16, quad, pl, pl_tk, tok0, NH)
                    fw.op("act", lambda e, pl=pl: e.copy(out=low_t[0:16, :], in_=pl[0:16, 0:NH]), outs=[K_["Em"]], ins=[pl_tk], cost=0.45)
                    pz, pz_tk = next_pab()
                    PE([(pz[:, 0:NH], gkw_bf[0:16, d, pr * 128:(pr + 1) * 128], low_t[0:16, :], True, True)], [pz_tk], [gkw_tk, K_["Em"]])
                    ACT(gl_t, pz[:, 0:NH], AF.Tanh, [K_["gl"]], [pz_tk, lb_tk], scale=0.5, bias=lbt[:, 12 + 2 * d + pr:13 + 2 * d + pr])
                    yield
                    ACT(gl_t, gl_t, AF.Ln, [K_["gl"]], [K_["gl"], cst_tk], scale=0.5, bias=half_ap)
                    ACT(sig_t, gl_t, AF.Exp, [K_["sig"]], [K_["gl"]], scale=1.0 / 16.0)
                    pk, pk_tk = next_pab()
                    proj_fm(W, wtk, 128, 128, quad, pk, pk_tk, tok0, NH)
                    fw.op("act", lambda e, pk=pk: e.copy(out=kk_t, in_=pk[:, 0:NH]), outs=[K_["kk"]], ins=[pk_tk], cost=0.45)
                    yield
                    pq, pq_tk = next_pab()
                    proj_fm(W, wtk, 0, 128, quad, pq, pq_tk, tok0, NH)
                    ACT(q_t, pq[:, 0:NH], AF.Copy, [K_["q"]], [pq_tk], scale=0.125)
                yield
                if d == 0:
                    fw.op("dve", lambda e: e.tensor_tensor_scan(out=Ep_t, data0=rmask[0][:, 0:NH], data1=sig_t, initial=1.0, op0=ALU.max, op1=ALU.mult),
                          outs=[K_["Ep"]], ins=[K_["sig"], consts_tk], cost=0.7)
                else:
                    fw.op("dve", lambda e: e.tensor_tensor_scan(out=Ep_t[:, ::-1], data0=rmask[1][:, 0:NH][:, ::-1], data1=sig_t[:, ::-1], initial=1.0,
                                                                op0=ALU.max, op1=ALU.mult), outs=[K_["Ep"]], ins=[K_["sig"], consts_tk], cost=0.7)
                yield
                TS("dve", Em_t, Ep_t, PMIN, None, ALU.max, None, [K_["Em"]], [K_["Ep"]])
                fw.op("dve", lambda e: e.reciprocal(out=Em_t, in_=Em_t), outs=[K_["Em"]], ins=[K_["Em"]], cost=1.5)
                yield
                TT("dve", kk_t, kk_t, Em_t, ALU.mult, [K_["kk"]], [K_["kk"], K_["Em"]])
                yield
                fw.op("act", lambda e: e.copy(out=kt_t[:, h0:h0 + NH], in_=kk_t), outs=[ptk["kt"][half]], ins=[K_["kk"]], cost=0.65)
                TT("dve", qt_t[:, h0:h0 + NH], q_t, Ep_t, ALU.mult, [ptk["qt"][half]], [K_["q"], K_["Ep"]])
                Ep3 = Ep_t.rearrange("p (c t) -> p c t", t=64)
                gc = 63 if d == 0 else 0
                nck = NH // 64
                for cc in range(nck):
                    c = half * nck + cc
                    ACT(kd_t[:, c * 64:(c + 1) * 64], kk_t[:, cc * 64:(cc + 1) * 64], AF.Identity, [ptk["kd"][c]], [K_["kk"], K_["Ep"]],
                        scale=Ep_t[:, cc * 64 + gc:cc * 64 + gc + 1])
                    if cc % 2 == 1:
                        yield
                fw.op("dve", lambda e: e.tensor_copy(out=Gq[d][buf][:, half * nck:half * nck + nck], in_=Ep3[:, :, gc]), outs=[Gq_tk[d][buf][half]], ins=[K_["Ep"]])

            def hg_params(hg):
                slot = hg % 2
                is_h = hg < 4
                if is_h:
                    return Wb[slot], W_tk[slot], is_h, [(0, 128, 0)], 128, 128
                return Wb[slot], W_tk[slot], is_h, [(0, 64, 0), (64, 64, 128)], 256, 256

            def gates_dir(hg, step, d):
                W, wtk, is_h, subs, vcol, nv = hg_params(hg)
                nhalf_ = 512 // NH
                for half in (list(range(nhalf_)) if d == 0 else list(range(nhalf_))[::-1]):
                    yield from gates_l0(hg, W, wtk, d, [step, 2 - step][d], half, step % 2)

            def scan_chain(hg, step, d):
                W, wtk, is_h, subs, vcol, nv = hg_params(hg)
                pr = hg - 4
                buf = step % 2
                quad = [step, 2 - step][d]
                blks = []
                for i in range(4):
                    b = quad * 4 + (i if d == 0 else 3 - i)
                    lo = (b % 4) * 128
                    G_of = (lambda r0, d=d, buf=buf, lo=lo: Gq[d][buf][:, (lo + r0) // 64:(lo + r0) // 64 + 1])
                    if is_h:
                        st_dst = (lambda seg, d=d, hg=hg: sth_d[seg, d, hg])
                        s0_src = s0h_d[d, hg]
                    else:
                        st_dst = (lambda seg, d=d, pr=pr: stg_d[seg, d, 2 * pr:2 * pr + 2].rearrange("h k v -> (h k) v"))
                        s0_src = s0g_d[d, 2 * pr:2 * pr + 2].rearrange("h k v -> (h k) v")
                    first = o_seen[hg]
                    qt_t, kt_t, kd_t = Pq[d][buf]
                    ptk = Pq_tk[d][buf]
                    c0 = lo // 64
                    hf_ = lo // NH
                    blks.append(Blk(b, d, subs, qt_t, kt_t, kd_t, [ptk["qt"][hf_], ptk["kt"][hf_], ptk["kd"][c0], ptk["kd"][c0 + 1]], lo, G_of,
                                    [Gq_tk[d][buf][hf_]], st_dst, s0_src, first, hg % 2))
                return scan_chain_gen(blks)

            o_seen = {hg_: set() for hg_ in range(8)}

            def gen_V0(hg):
                W, wtk, is_h, subs, vcol, nv = hg_params(hg)
                return gen_V(W, wtk, vcol, nv, hg % 2)

            fw.fence(arena_tks)
            run_interleaved([gen_V0(0), gates_dir(0, 0, 0), gates_dir(0, 0, 1)])
            for hg in range(6):
                W, wtk, is_h, subs, vcol, nv = hg_params(hg)
                pr = hg - 4
                for step in range(3):
                    gens = [scan_chain(hg, step, 0), scan_chain(hg, step, 1)]
                    if step + 1 < 3:
                        gens.append(gates_dir(hg, step + 1, 0))
                        gens.append(gates_dir(hg, step + 1, 1))
                    if step == 0 and hg + 1 < 6:
                        gens.append(gen_V0(hg + 1))
                    run_interleaved(gens)
                    if step == 1 and hg + 2 < 6:
                        load_hg_l0(hg % 2, hg + 2, "early", fin_groups_l0(hg))
                if is_h:
                    fin = finalize(W, wtk, [(0, 512, hg, 64 + hg)], on_bf0, [on_tk0])
                else:
                    fin = finalize(W, wtk, [(0, 512, 4 + 2 * pr, 64 + 4 + 2 * pr), (128, 640, 5 + 2 * pr, 64 + 5 + 2 * pr)], on_bf0, [on_tk0])
                gens = [fin]
                if hg + 1 < 6:
                    gens.append(gates_dir(hg + 1, 0, 0))
                    gens.append(gates_dir(hg + 1, 0, 1))
                run_interleaved(gens)
                if hg + 2 < 6:
                    load_hg_l0(hg % 2, hg + 2, "late", fin_groups_l0(hg))
                else:
                    load_cols(hg % 2, 0, w_out_e_d[0], (hg % 2) * 512, 512, True)
                checkpoint('hg%d' % hg)
            out_proj(0, w_out_e_d[0], last=False)
            checkpoint('out0')

            adaln(1, preloaded=5)
            load_hg_l1(0, 0)
            load_hg_l1(1, 1)
            norm_mod(1)
            checkpoint('norm1')

            qk = [ar(i * 3 * KB, T, BF16) for i in range(4)]
            qk_tk = [Tk("qk%d" % i) for i in range(4)]
            xbf_b = [ar(12 * KB, 512, BF16), ar(27 * KB + 512, 512, BF16)]
            t1_b = [ar(13 * KB, 512), ar(28 * KB + 512, 512)]
            t2_b = [ar(15 * KB, 512), ar(30 * KB + 512, 512)]
            cs_t = [ar(17 * KB + j * 2 * KB, 512) for j in range(2)]
            EpEm = ar(21 * KB, 512)
            gam_t = ar(23 * KB, 128)
            m128 = ar(23 * KB + 512, 256)
            xbf_tks, t1_tks, t2_tks = [Tk("xbf"), Tk("xbf")], [Tk("t1"), Tk("t1")], [Tk("t2"), Tk("t2")]
            EpEm_tk, gam_tk, m128_tk = Tk("EpEm"), Tk("gam"), Tk("m128")
            on_bf1 = ar(24 * KB + 512, T, BF16).rearrange("p (b v) -> p b v", v=128)
            on_tk1 = Tk("on1")
            cs_tk = Tk("cs")
            arena_tks += qk_tk + xbf_tks + t1_tks + t2_tks + [EpEm_tk, gam_tk, m128_tk, on_tk1, cs_tk]
            fw.fence(arena_tks)
            ones128 = consts[:, C_CF:C_CF + 128]
            KSC = 128.0 ** -0.5
            LOAD("sp", "c8", m128, m128_d[:, :], [m128_tk])
            maskT128 = [m128[:, 0:128], m128[:, 128:256]]

            subs1 = [(0, 128, 0)]

            pcount = [0]

            def prologue1(h):
                slot = h % 2
                W = Wb[slot]
                wtk = W_tk[slot]
                for d in range(2):
                    TS("dve", gam_t, ones128, retd[:, 16 + d * 8 + h:17 + d * 8 + h], None, ALU.mult, None, [gam_tk], [consts_tk, retd_tk])
                    Epd = EpEm[:, d * 128:(d + 1) * 128]
                    if d == 0:
                        fw.op("dve", lambda e, Epd=Epd: e.tensor_tensor_scan(out=Epd, data0=gam_t, data1=ones128, initial=1.0, op0=ALU.mult, op1=ALU.min),
                              outs=[EpEm_tk], ins=[gam_tk, consts_tk])
                    else:
                        fw.op("dve", lambda e, Epd=Epd: e.tensor_tensor_scan(out=Epd[:, ::-1], data0=gam_t[:, ::-1], data1=ones128[:, ::-1], initial=1.0,
                                                                             op0=ALU.mult, op1=ALU.min), outs=[EpEm_tk], ins=[gam_tk, consts_tk])
                    fw.op("dve", lambda e, Epd=Epd, d=d: e.reciprocal(out=EpEm[:, 256 + d * 128:256 + (d + 1) * 128], in_=Epd), outs=[EpEm_tk], ins=[EpEm_tk])
                    TS("dve", EpEm[:, 256 + d * 128:256 + (d + 1) * 128], EpEm[:, 256 + d * 128:256 + (d + 1) * 128], KSC, None, ALU.mult, None, [EpEm_tk], [EpEm_tk])
                    yield
                for quad in range(3):
                    LOAD("sp", "cs", cs_t[0], cos_d[:, quad * 512:(quad + 1) * 512], [cs_tk])
                    LOAD("sp", "cs", cs_t[1], sin_d[:, quad * 512:(quad + 1) * 512], [cs_tk], fresh=False)
                    for which in range(2):
                        pi_ = pcount[0] % 2
                        pcount[0] += 1
                        xbf_t, t1_t, t2_t = xbf_b[pi_], t1_b[pi_], t2_b[pi_]
                        xbf_tk, t1_tk, t2_tk = xbf_tks[pi_], t1_tks[pi_], t2_tks[pi_]
                        pq, pq_tk = next_pab()
                        proj_fm(W, wtk, 128 * which, 128, quad, pq, pq_tk)
                        fw.op("act", lambda e, pq=pq, xbf_t=xbf_t: e.copy(out=xbf_t, in_=pq[:, 0:512]), outs=[xbf_tk], ins=[pq_tk], cost=0.65)
                        pr_, pr_tk = next_pab()
                        PE([(pr_[:, 0:512], Rb[:, :], xbf_t, True, True)], [pr_tk], [cb_tk, xbf_tk])
                        TT("dve", t1_t, pq[:, 0:512], cs_t[0], ALU.mult, [t1_tk], [pq_tk, cs_tk])
                        TT("dve", t2_t, pr_[:, 0:512], cs_t[1], ALU.mult, [t2_tk], [pr_tk, cs_tk])
                        yield
                        TT("dve", t1_t, t1_t, t2_t, ALU.add, [t1_tk], [t1_tk, t2_tk])
                        yield
                        t1v = t1_t.rearrange("p (c t) -> p c t", t=128)
                        for d in range(2):
                            dst = qk[2 * d + which][:, quad * 512:(quad + 1) * 512].rearrange("p (c t) -> p c t", t=128)
                            if which == 0:
                                e_bc = EpEm[:, d * 128:(d + 1) * 128].unsqueeze(1).to_broadcast([128, 4, 128])
                                TT("pool" if d == 0 else "dve", dst, t1v, e_bc, ALU.mult, [qk_tk[2 * d + which]], [t1_tk, EpEm_tk])
                            else:
                                e_bc = EpEm[:, 256 + d * 128:256 + (d + 1) * 128].unsqueeze(1).to_broadcast([128, 4, 128])
                                TT("dve" if d == 0 else "pool", dst, t1v, e_bc, ALU.mult, [qk_tk[2 * d + which]], [t1_tk, EpEm_tk])
                        yield

            def scan_chain1(h, d):
                blks = []
                for i in range(NB):
                    b = i if d == 0 else NB - 1 - i
                    gcol = 127 if d == 0 else 128
                    G_of = (lambda r0, gcol=gcol: EpEm[:, gcol:gcol + 1])
                    st_dst = (lambda seg, d=d, h=h: str_d[seg, d, h])
                    first = o_seen1[h]
                    blks.append(Blk(b, d, subs1, qk[2 * d], qk[2 * d + 1], qk[2 * d + 1], [qk_tk[2 * d], qk_tk[2 * d + 1]], b * 128, G_of, [EpEm_tk],
                                    st_dst, s0r_d[d, h], first, h % 2, ktok_scale=EpEm[:, gcol:gcol + 1], nch=1, maskT=maskT128, mask_tks=[m128_tk]))
                return scan_chain_gen(blks)

            o_seen1 = {h_: set() for h_ in range(8)}

            def gen_V1(h):
                return gen_V(Wb[h % 2], W_tk[h % 2], 256, 128, h % 2)

            run_interleaved([chain([gen_V1(0), prologue1(0)])])
            for h in range(8):
                slot = h % 2
                W = Wb[slot]
                wtk = W_tk[slot]
                if h + 2 < 8:
                    load_hg_l1(slot, h + 2, "early")
                gens = [scan_chain1(h, 0), scan_chain1(h, 1)]
                if h + 1 < 8:
                    gens.append(gen_V1(h + 1))
                run_interleaved(gens, [3, 3, 1][:len(gens)])
                gens = [finalize(W, wtk, [(0, 384, h, 72 + h)], on_bf1, [on_tk1])]
                if h + 1 < 8:
                    gens.append(prologue1(h + 1))
                run_interleaved(gens, [1, 3][:len(gens)])
                if h + 2 < 8:
                    load_hg_l1(slot, h + 2, "late")
                else:
                    load_cols(slot, 0, w_out_o_d[0], slot * 512, 512, True)
                checkpoint('rh%d' % h)
            out_proj(1, w_out_o_d[0], last=True)
        except _Stop:
            pass

        fw.wait_all("sp", [(k, fw.dma_cnt[k]) for k in fw.dma_cnt])
        print("recorded instructions:", fw.n_ins)
        fw.replay()
    return nc


def _consts():
    c = np.zeros((128, NCONST), np.float32)
    c[:, C_ID:C_ID + 128] = np.eye(128, dtype=np.float32)
    s = np.arange(128)[:, None]
    t = np.arange(128)[None, :]
    same = (s // 64) == (t // 64)
    c[:, C_MF:C_MF + 128] = (same & (s <= t)).astype(np.float32)
    c[:, C_MB:C_MB + 128] = (same & (s >= t)).astype(np.float32)
    R = np.zeros((128, 128), np.float32)
    for j in range(32):
        R[32 + j, j] = -1.0
        R[j, 32 + j] = 1.0
        R[96 + j, 64 + j] = -1.0
        R[64 + j, 96 + j] = 1.0
    c[:, C_R:C_R + 128] = R
    tt = np.arange(512)
    c[:, C_SF:C_SF + 512] = (tt % 64 == 0).astype(np.float32)[None, :]
    c[:, C_SB:C_SB + 512] = (tt % 64 == 63).astype(np.float32)[None, :]
    c[:, C_CF:C_CF + 128] = 1.0
    for seg in range(NSEG):
        c[seg, C_SEL + seg * 128:C_SEL + (seg + 1) * 128] = 1.0
    return c


def _rope_tables():
    tpos = np.arange(1024)
    row = (tpos // 64).astype(np.float32)
    col = (tpos % 64).astype(np.float32)
    half = 64
    inv = (10000.0 ** (-np.arange(0, half, 2, dtype=np.float32) / half)).astype(np.float32)
    ang_r = row[:, None] * inv[None, :]
    ang_c = col[:, None] * inv[None, :]
    ang = np.concatenate([ang_r, ang_r, ang_c, ang_c], axis=-1)
    return np.cos(ang).astype(np.float32).T.copy(), np.sin(ang).astype(np.float32).T.copy()


_PROGRAM = None


def _core_layout():
    lay = []
    for c in range(4):
        lay.append([("s", c, j) for j in range(4)] + [("p", 2 * c), ("p", 2 * c + 1)])
    for c in range(4, 8):
        lay.append([("p", 8 + 6 * (c - 4) + j) for j in range(6)])
    return lay


def kernel(x_prompt, x_sample, state_hgrn, state_gla, state_ret, c, c_ctx, norm_w, ada_w, ada_b,
           w_in_even, hgrn_lb, gla_gk_w, gla_gk_b, gn_even, w_out_even, w_in_odd, ret_decay, gn_odd,
           w_out_odd, final_norm_w):
    global _PROGRAM
    f32 = lambda a: np.ascontiguousarray(np.asarray(a, dtype=np.float32))
    x_prompt, x_sample = f32(x_prompt), f32(x_sample)
    state_hgrn, state_gla, state_ret = f32(state_hgrn), f32(state_gla), f32(state_ret)
    c, c_ctx = f32(c), f32(c_ctx)
    if _PROGRAM is None:
        _PROGRAM = build_program()
    nc = _PROGRAM
    consts = _consts()
    cosT, sinT = _rope_tables()
    lay = _core_layout()
    s_ = np.arange(128)[:, None]
    t_ = np.arange(128)[None, :]
    m128 = np.concatenate([(s_ <= t_).astype(np.float32), (s_ >= t_).astype(np.float32)], axis=1)
    shared = dict(consts=consts, m128=m128, norm_w=f32(norm_w), ada_w=f32(ada_w), ada_b=f32(ada_b), w_in_even=f32(w_in_even),
                  hgrn_lb=f32(hgrn_lb), gla_gk_w=f32(gla_gk_w), gla_gk_b=f32(gla_gk_b), gn_even=f32(gn_even),
                  w_out_even=f32(w_out_even), w_in_odd=f32(w_in_odd), ret_decay=f32(ret_decay), gn_odd=f32(gn_odd),
                  w_out_odd=f32(w_out_odd), final_norm_w=f32(final_norm_w))
    in_maps = []
    for ci in range(NCORES):
        xs = np.empty((T, D), np.float32)
        cond = np.empty((NSEG, D), np.float32)
        cs = np.ones((128, T), np.float32)
        sn = np.zeros((128, T), np.float32)
        for si, ent in enumerate(lay[ci]):
            if ent[0] == "s":
                xs[si * 256:(si + 1) * 256] = x_sample[ent[1], ent[2] * 256:(ent[2] + 1) * 256]
                cond[si] = c[ent[1]]
                cs[:, si * 256:(si + 1) * 256] = cosT[:, ent[2] * 256:(ent[2] + 1) * 256]
                sn[:, si * 256:(si + 1) * 256] = sinT[:, ent[2] * 256:(ent[2] + 1) * 256]
            else:
                xs[si * 256:(si + 1) * 256] = x_prompt[ent[1]]
                cond[si] = c_ctx
        if ci < 4:
            flag = np.ones((128, 1), np.float32)
            s0h = state_hgrn[ci, 0]
            s0g = state_gla[ci, 0]
            s0r = state_ret[ci, 0]
        else:
            flag = np.zeros((128, 1), np.float32)
            s0h = np.zeros((2, 4, 128, 128), np.float32)
            s0g = np.zeros((2, 4, 64, 128), np.float32)
            s0r = np.zeros((2, 8, 128, 128), np.float32)
        m = dict(shared)
        m.update(x=xs, cond=cond, flag=flag, s0h=np.ascontiguousarray(s0h), s0g=np.ascontiguousarray(s0g),
                 s0r=np.ascontiguousarray(s0r), cosT=cs, sinT=sn)
        in_maps.append(m)
    res = run_bass_kernel_spmd(nc, in_maps, core_ids=list(range(NCORES)))
    outs = res.results
    B = x_prompt.shape[0]
    y_prompt = np.empty((B, 256, D), np.float32)
    y_sample = np.empty((x_sample.shape[0], 1024, D), np.float32)
    nsh = np.empty((B, 1, 2, 4, 128, 128), np.float32)
    nsg = np.empty((B, 1, 2, 4, 64, 128), np.float32)
    nsr = np.empty((B, 1, 2, 8, 128, 128), np.float32)
    for ci in range(NCORES):
        r = outs[ci]
        for si, ent in enumerate(lay[ci]):
            yb = r["y"][si * 256:(si + 1) * 256]
            if ent[0] == "s":
                y_sample[ent[1], ent[2] * 256:(ent[2] + 1) * 256] = yb
            else:
                y_prompt[ent[1]] = yb
                nsh[ent[1], 0] = r["st_h"][si]
                nsg[ent[1], 0] = r["st_g"][si]
                nsr[ent[1], 0] = r["st_r"][si]
    return (y_prompt, y_sample, nsh, nsg, nsr)
```

```python
import numpy as np
from contextlib import ExitStack
import concourse.bass as bass
import concourse.mybir as mybir
from concourse.bass_utils import run_bass_kernel_spmd

F32 = mybir.dt.float32
BF16 = mybir.dt.bfloat16
AF = mybir.ActivationFunctionType
ALU = mybir.AluOpType

NCORES = 8
D = 1024
T = 1536
NB = 12
NSEG = 6
EPS = 1e-6
CL_H = -80.0
CL_G = -80.0 * 16.0

C_ID = 0
C_MF = 128
C_MB = 256
C_R = 384
C_SF = 512
C_SB = 1024
C_CF = 1536
C_CB = 1600
C_SEL = 1664
NCONST = 1664 + 768


class Tk:
    __slots__ = ("name", "w", "r", "x")

    def __init__(self, name="", x=False):
        self.name = name
        self.w = []
        self.r = []
        self.x = x


class EngW:
    def __init__(self, name):
        self.name = name
        self.cnt = 0
        self.seen = {}
        self.recs = []


class FW:
    def __init__(self, nc, stack, n_dma_sems=48):
        self.nc = nc
        self.sems = {}
        self.E = {}
        for nm in ("pe", "act", "dve", "pool", "sp"):
            self.sems[nm] = stack.enter_context(nc.semaphore("s_" + nm))
            self.E[nm] = EngW(nm)
        self.dma_free = []
        for i in range(n_dma_sems):
            self.sems["d%d" % i] = stack.enter_context(nc.semaphore("d%d" % i))
            self.dma_free.append("d%d" % i)
        self.dma_cnt = {}
        self.dma_named = {}
        self.n_ins = 0
        self.t_free = {nm: 0.0 for nm in self.E}
        self.t_done = {}
        self.last_end = 0.0

    def _dep_time(self, outs, ins, force=()):
        t = 0.0
        td = self.t_done
        for tk in ins:
            for tok in tk.w:
                t = max(t, td.get(tok, 0.0))
        for tk in outs:
            for tok in tk.w:
                t = max(t, td.get(tok, 0.0))
            for tok in tk.r:
                t = max(t, td.get(tok, 0.0))
        for tok in force:
            t = max(t, td.get(tok, 0.0))
        return t

    def _waits(self, e, outs, ins):
        deps = []
        for t in ins:
            deps.extend(t.w)
        for t in outs:
            deps.extend(t.w)
            deps.extend(t.r)
        best = {}
        for (k, v) in deps:
            if k == "pe" and e.name == "pe":
                continue
            if v <= e.seen.get(k, 0):
                continue
            if v > best.get(k, 0):
                best[k] = v
        for k, v in best.items():
            e.seen[k] = v
        return list(best.items())

    def op(self, eng, fn, outs=(), ins=(), force=(), cost=0.3):
        e = self.E[eng]
        outs = list(outs) + [t for t in ins if t.x]
        ins = [t for t in ins if not t.x]
        t0 = max(self.t_free[eng], self._dep_time(outs, ins, force) + 0.1)
        t1 = t0 + cost
        self.t_free[eng] = t1
        self.last_end = max(self.last_end, t0)
        waits = self._waits(e, outs, ins)
        for (k, v) in force:
            if v > e.seen.get(k, 0):
                e.seen[k] = v
                waits.append((k, v))
        e.cnt += 1
        tok = (e.name, e.cnt)
        self.t_done[tok] = t1
        e.recs.append((waits, fn, [(e.name, 1)]))
        for t in ins:
            t.r.append(tok)
        for t in outs:
            t.w = [tok]
            t.r = []
        self.n_ins += 1
        return tok

    def dma_sem(self, name):
        if name not in self.dma_named:
            k = self.dma_free.pop()
            self.dma_named[name] = k
            self.dma_cnt[k] = 0
        return self.dma_named[name]

    def dma(self, queue, semname, fn, outs=(), ins=(), fresh=True, nbytes=65536):
        e = self.E[queue]
        k = self.dma_sem(semname)
        t0 = max(self.t_free[queue], self._dep_time(outs, ins) + 0.1)
        self.t_free[queue] = t0 + (1.0 if queue == "pool" else 0.1)
        t1 = t0 + 2.0 + nbytes / 200e3
        waits = self._waits(e, outs, ins) if fresh else []
        self.dma_cnt[k] += 16
        tok = (k, self.dma_cnt[k])
        self.t_done[tok] = t1
        e.recs.append((waits, fn, [(k, 16)]))
        for t in ins:
            t.r.append(tok)
        for t in outs:
            if fresh:
                t.w = [tok]
                t.r = []
            else:
                t.w.append(tok)
        self.n_ins += 1
        return tok

    def fence(self, tks):
        best = {}
        for t in tks:
            for (k, v) in list(t.w) + list(t.r):
                if v > best.get(k, 0):
                    best[k] = v
        for q in self.E:
            self.wait_all(q, list(best.items()))

    def wait_all(self, queue, toks):
        e = self.E[queue]
        waits = []
        for (k, v) in toks:
            if v > e.seen.get(k, 0):
                e.seen[k] = v
                waits.append((k, v))
        if waits:
            e.recs.append((waits, None, []))

    def replay(self):
        nc = self.nc
        sems = self.sems

        def run(e):
            def body(eng):
                for (waits, fn, incs) in e.recs:
                    for (k, v) in waits:
                        eng.wait_ge(sems[k], v)
                    if fn is None:
                        continue
                    ins = fn(eng)
                    for (k, n) in incs:
                        ins.then_inc(sems[k], n)
            return body

        with nc.Block() as block:
            block.tensor(run(self.E["pe"]))
            block.scalar(run(self.E["act"]))
            block.vector(run(self.E["dve"]))
            block.gpsimd(run(self.E["pool"]))
            block.sync(run(self.E["sp"]))


class _Stop(Exception):
    pass


STOP_AT = None


def checkpoint(name):
    if STOP_AT == name:
        raise _Stop()


def build_program(debug=False):
    nc = bass.Bass("TRN2", target_bir_lowering=False)

    def din(name, shape):
        return nc.dram_tensor(name, list(shape), F32, kind="ExternalInput").ap()

    def dout(name, shape):
        return nc.dram_tensor(name, list(shape), F32, kind="ExternalOutput").ap()

    x_d = din("x", [T, D])
    cond_d = din("cond", [NSEG, D])
    flag_d = din("flag", [128, 1])
    s0h_d = din("s0h", [2, 4, 128, 128])
    s0g_d = din("s0g", [2, 4, 64, 128])
    s0r_d = din("s0r", [2, 8, 128, 128])
    cos_d = din("cosT", [128, T])
    sin_d = din("sinT", [128, T])
    consts_d = din("consts", [128, NCONST])
    m128_d = din("m128", [128, 256])
    norm_w_d = din("norm_w", [2, D])
    ada_w_d = din("ada_w", [2, D, 3 * D])
    ada_b_d = din("ada_b", [2, 3 * D])
    w_in_e_d = din("w_in_even", [1, D, 4128])
    hgrn_lb_d = din("hgrn_lb", [2, 512])
    gk_w_d = din("gla_gk_w", [1, 2, 16, 256])
    gk_b_d = din("gla_gk_b", [1, 2, 256])
    gn_e_d = din("gn_even", [1, D])
    w_out_e_d = din("w_out_even", [1, D, D])
    w_in_o_d = din("w_in_odd", [1, D, 4096])
    ret_dec_d = din("ret_decay", [1, 2, 8])
    gn_o_d = din("gn_odd", [1, D])
    w_out_o_d = din("w_out_odd", [1, D, D])
    fnw_d = din("final_norm_w", [D])

    y_d = dout("y", [T, D])
    sth_d = dout("st_h", [NSEG, 2, 4, 128, 128])
    stg_d = dout("st_g", [NSEG, 2, 4, 64, 128])
    str_d = dout("st_r", [NSEG, 2, 8, 128, 128])

    with ExitStack() as st:
        fw = FW(nc, st, n_dma_sems=48)

        def sb(name, shape, dt=F32):
            return st.enter_context(nc.sbuf_tensor(name, list(shape), dt))

        def ps(name, shape, dt=F32):
            return st.enter_context(nc.psum_tensor(name, list(shape), dt))

        x_sb = sb("x_sb", [128, NB, D])
        x_tk = [Tk("x%d" % b) for b in range(NB)]
        hT = sb("hT", [128, 8, T], BF16)
        hT_tk = [[Tk("hT%d_%d" % (b, kc)) for kc in range(8)] for b in range(NB)]
        ogT = sb("ogT", [128, 8, T], BF16)
        og_tk = [Tk("og%d" % h) for h in range(8)]
        Wb = [sb("W0", [128, 8, 800], BF16), sb("W1", [128, 8, 800], BF16)]
        W_tk = [[Tk("W0g%d" % g) for g in range(7)], [Tk("W1g%d" % g) for g in range(7)]]

        def wsel(wtk, c0, ncol):
            return wtk[c0 // 128:(c0 + ncol - 1) // 128 + 1]
        o_sb = sb("o_sb", [128, NB, 256])
        o_tk = [Tk("o%d" % b) for b in range(NB)]
        v_sbs = [sb("v_sb0", [128, NB, 256], BF16), sb("v_sb1", [128, NB, 256], BF16)]
        v_tks = [[Tk("v%d" % b) for b in range(NB)] for i in range(2)]
        AR_BYTES = 32 * 1024 + 512
        arena = sb("arena", [128, AR_BYTES // 4])
        consts = sb("consts_sb", [128, NCONST])
        consts_tk = Tk("consts")
        identb = sb("identb", [128, 128], BF16)
        Rb = sb("Rb", [128, 128], BF16)
        cb_tk = Tk("cb")
        colsA = sb("colsA", [128, 96])
        colsB = sb("colsB", [128, 32])
        cols_tk = Tk("cols")
        scondT = sb("scondT", [128, 48], BF16)
        scond_tk = Tk("scond")
        lbt = sb("lbt", [128, 16])
        lb_tk = Tk("lb")
        flag_sb = sb("flag_sb", [128, 1])
        flag_tk = Tk("flag")
        retd = sb("retd", [128, 48])
        retd_tk = Tk("retd")
        gkw_bf = sb("gkw_bf", [16, 2, 256], BF16)
        gkw_tk = Tk("gkw")
        modT = sb("modT", [128, 16, NSEG])
        mod_tk = Tk("modT")
        A_l = sb("A_l", [128, 8, NSEG])
        A_tk = Tk("A")
        grow = sb("grow", [NSEG, D])
        grow_tk = Tk("grow")
        tmp = arena[:, 0:1024]
        tmp_tk = Tk("tmp")
        xn_bf = arena[:, 1024:1536].bitcast(BF16)
        gbc = arena[:, 1536:2560]
        gbc_tk = Tk("gbc")
        fnw_bc = arena[:, 2560:3584]
        fnw_tk = Tk("fnw")
        arena_tks = [tmp_tk, gbc_tk, fnw_tk]
        stageA = tmp[:, 0:128]
        stageB = tmp[0:32, 128:256]
        xn_tk = Tk("xn")
        arena_tks.append(xn_tk)
        stat = sb("stat", [128, 64])
        stat_tk = Tk("stat")
        S32 = [[sb("S32_%d_%d" % (d, i), [128, 128]) for i in range(4)] for d in range(2)]
        S32_tk = [[Tk("S32") for i in range(4)] for d in range(2)]
        Sbf = [[sb("Sbf_%d_%d" % (d, i), [128, 128], BF16) for i in range(4)] for d in range(2)]
        Sbf_tk = [[Tk("Sbf") for i in range(4)] for d in range(2)]
        PT = [[[sb("PT_%d_%d_%d" % (d, i, j), [128, 128], BF16) for j in range(2)] for i in range(2)] for d in range(2)]
        PT_tk = [[[Tk("PT") for j in range(2)] for i in range(2)] for d in range(2)]
        ktok = [sb("ktok_%d" % i, [128, 128], BF16) for i in range(2)]
        ktok_tk = [Tk("ktok") for i in range(2)]
        Gq = [[sb("Gq_%d_%d" % (d, i), [128, 8]) for i in range(2)] for d in range(2)]
        sg_q = sb("sg_q", [128, 512], BF16)
        sgq_tk = Tk("sgq")

        print("SBUF bytes remaining/partition:", nc.sbuf_bytes_remaining)

        PA = ps("PA", [128, 512])
        PB = ps("PB", [128, 512])
        PAB = [PA, PB]
        PAB_tk = [Tk("PA", True), Tk("PB", True)]
        Y = ps("Y", [128, 1024])
        Yh_tk = [Tk("Y0", True), Tk("Y1", True)]
        O = [ps("O0", [128, 512]), ps("O1", [128, 512])]
        O_tk = [Tk("O0", True), Tk("O1", True)]
        OB = [[(O[0][:, :], O_tk[0]), (O[1][:, :], O_tk[1])],
              [(Y[:, 0:512], Yh_tk[0]), (Y[:, 512:1024], Yh_tk[1])]]
        M = [ps("M0", [128, 512]), ps("M1", [128, 512])]
        M_tk = [Tk("M0", True), Tk("M1", True)]
        pab_i = [0]

        def next_pab():
            i = pab_i[0]
            pab_i[0] ^= 1
            return PAB[i], PAB_tk[i]

        def fsz(ap):
            n = 1
            for d_ in ap.shape[1:]:
                n *= int(d_)
            return n

        def ecost(eng, ap):
            f = fsz(ap)
            if eng == "act":
                return 0.22 + f / 1200.0
            if eng == "pool":
                return 0.15 + f * 2.6e-3
            return 0.2 + f / 960.0

        def ACT(out, in_, func, outs, ins, **kw):
            fw.op("act", lambda e: e.activation(out=out, in_=in_, func=func, **kw), outs=outs, ins=ins, cost=ecost("act", out))

        def TT(eng, out, in0, in1, op, outs, ins):
            fw.op(eng, lambda e: e.tensor_tensor(out=out, in0=in0, in1=in1, op=op), outs=outs, ins=ins, cost=ecost(eng, out))

        def TS(eng, out, in0, s1, s2, op0, op1, outs, ins):
            if s2 is None:
                fw.op(eng, lambda e: e.tensor_scalar(out=out, in0=in0, scalar1=s1, scalar2=None, op0=op0), outs=outs, ins=ins, cost=ecost(eng, out))
            else:
                fw.op(eng, lambda e: e.tensor_scalar(out=out, in0=in0, scalar1=s1, scalar2=s2, op0=op0, op1=op1), outs=outs, ins=ins, cost=ecost(eng, out))

        def STT(out, in0, scalar, in1, op0, op1, outs, ins):
            fw.op("dve", lambda e: e.scalar_tensor_tensor(out=out, in0=in0, scalar=scalar, in1=in1, op0=op0, op1=op1), outs=outs, ins=ins,
                  cost=ecost("dve", out))

        def PE(mms, outs, ins, force=()):
            def fn(e):
                r = None
                for (o, l, rh, s1, s2) in mms:
                    r = e.matmul(o, lhsT=l, rhs=rh, start=s1, stop=s2)
                return r
            c = 0.15
            for (o, l, rh, s1, s2) in mms:
                c += max(0.06, fsz(rh) / 2400.0 + 0.01)
            return fw.op("pe", fn, outs=outs, ins=ins, force=force, cost=c)

        def nbytes(ap):
            n = 4
            for d_ in ap.shape:
                n *= int(d_)
            return n

        def LOAD(queue, sem, out, in_, outs, fresh=True):
            fw.dma(queue, sem, lambda e: e.dma_start(out=out, in_=in_), outs=outs, fresh=fresh, nbytes=nbytes(out))

        def STORE(queue, sem, out, in_, ins):
            fw.dma(queue, sem, lambda e: e.dma_start(out=out, in_=in_), ins=ins, nbytes=nbytes(in_))

        def w_view(w2d):
            return w2d.rearrange("(kc p) n -> p kc n", p=128)

        def load_cols(slot, off, src2d, c0, ncol, fresh):
            src = w_view(src2d)[:, :, c0:c0 + ncol]
            LOAD("pool", "W%d_%d" % (slot, off // 128), Wb[slot][:, :, off:off + ncol], src, wsel(W_tk[slot], off, ncol))

        def cols_l0(hg):
            if hg < 4:
                h = hg
                return [(128 * j, c0, 128) for j, c0 in enumerate([128 * h, 512 + 128 * h, 1024 + 128 * h, 1536 + 128 * h, 2048 + 128 * h])]
            p = hg - 4
            return [(0, 2560 + 128 * p, 128), (128, 2816 + 128 * p, 128), (256, 3072 + 256 * p, 256), (512, 3584 + 256 * p, 256), (768, 4096, 32)]

        def fin_groups_l0(hg):
            return {4} if hg < 4 else {4, 5}

        def load_hg_l0(slot, hg, phase="all", busy=()):
            for (off, c0, ncol) in cols_l0(hg):
                groups = set(range(off // 128, (off + ncol - 1) // 128 + 1))
                hit = bool(groups & set(busy))
                if phase == "all" or (phase == "early" and not hit) or (phase == "late" and hit):
                    load_cols(slot, off, w_in_e_d[0], c0, ncol, True)

        def load_hg_l1(slot, h, phase="all"):
            w = w_in_o_d[0]
            for j in range(4):
                late = (j == 3)
                if phase == "all" or (phase == "early" and not late) or (phase == "late" and late):
                    load_cols(slot, 128 * j, w, 1024 * j + 128 * h, 128, True)

        LOAD("sp", "c0", consts[:, :], consts_d[:, :], [consts_tk])
        LOAD("sp", "c1", flag_sb[:, :], flag_d[:, :], [flag_tk])
        LOAD("sp", "c2", retd[:, 0:16], ret_dec_d[0].rearrange("a b -> (a b)").partition_broadcast(128), [retd_tk])
        stA_tk = tmp_tk
        LOAD("sp", "c4", stageA[0:48, :], cond_d.rearrange("s (kc j) -> (s kc) j", j=128), [stA_tk])
        LOAD("sp", "c4", stageA[48:64, :], norm_w_d.rearrange("l (kc j) -> (l kc) j", j=128), [stA_tk], fresh=False)
        LOAD("sp", "c4", stageA[64:72, :], gn_e_d.rearrange("l (kc j) -> (l kc) j", j=128), [stA_tk], fresh=False)
        LOAD("sp", "c4", stageA[72:80, :], gn_o_d.rearrange("l (kc j) -> (l kc) j", j=128), [stA_tk], fresh=False)
        LOAD("sp", "c4", stageA[80:88, :], hgrn_lb_d.rearrange("l (kc j) -> (l kc) j", j=128), [stA_tk], fresh=False)
        LOAD("sp", "c4", stageA[88:92, :], gk_b_d[0].rearrange("l (kc j) -> (l kc) j", j=128), [stA_tk], fresh=False)
        LOAD("pool", "c5", gkw_bf[:, :, :], gk_w_d[0].rearrange("d r n -> r d n"), [gkw_tk])
        for blk in range(NB):
            LOAD("sp", "x%d" % blk, x_sb[:, blk, :], x_d[blk * 128:(blk + 1) * 128, :], [x_tk[blk]])

        identf = consts[:, C_ID:C_ID + 128]
        TS("dve", identb[:, :], identf, 1.0, None, ALU.mult, None, [cb_tk], [consts_tk])
        TS("dve", Rb[:, :], consts[:, C_R:C_R + 128], 1.0, None, ALU.mult, None, [cb_tk], [consts_tk])

        pt, pt_tk = next_pab()
        PE([(pt[:, 0:92], stageA[0:92, :], consts[0:92, C_ID:C_ID + 92], True, True)], [pt_tk], [stA_tk, consts_tk])
        fw.op("act", lambda e, pt=pt: e.copy(out=colsA[:, 0:92], in_=pt[:, 0:92]), outs=[cols_tk], ins=[pt_tk])
        stB_tk = tmp_tk
        LOAD("sp", "c6", stageB[0:16, :], ada_b_d[0, 0:2048].rearrange("(kc j) -> kc j", j=128), [stB_tk])
        LOAD("sp", "c6", stageB[16:32, :], ada_b_d[1, 0:2048].rearrange("(kc j) -> kc j", j=128), [stB_tk], fresh=False)
        pt, pt_tk = next_pab()
        PE([(pt[:, 0:32], stageB[0:32, :], consts[0:32, C_ID:C_ID + 32], True, True)], [pt_tk], [stB_tk, consts_tk])
        fw.op("act", lambda e, pt=pt: e.copy(out=colsB[:, 0:32], in_=pt[:, 0:32]), outs=[cols_tk], ins=[pt_tk])
        ACT(scondT[:, :], colsA[:, 0:48], AF.Silu, [scond_tk], [cols_tk])
        TT("dve", lbt[:, 0:4], colsA[:, 80:84], colsA[:, 84:88], ALU.subtract, [lb_tk], [cols_tk])
        ACT(lbt[:, 0:4], lbt[:, 0:4], AF.Sigmoid, [lb_tk], [lb_tk])
        TS("dve", lbt[:, 4:8], lbt[:, 0:4], -0.5, 0.5, ALU.mult, ALU.add, [lb_tk], [lb_tk])
        TS("dve", lbt[:, 8:12], lbt[:, 0:4], 0.5, 0.5, ALU.mult, ALU.add, [lb_tk], [lb_tk])
        TS("dve", lbt[:, 12:16], colsA[:, 88:92], 0.5, None, ALU.mult, None, [lb_tk], [lb_tk, cols_tk])
        ACT(retd[:, 16:32], retd[:, 0:16], AF.Sigmoid, [retd_tk], [retd_tk])

        scond_k = scondT[:, :].rearrange("p (s k) -> p k s", k=8)

        state = {"s32": 0, "sbf": 0, "u": 0}

        Xb = [arena[:, 3584:5632].bitcast(BF16).rearrange("p (k n) -> p k n", n=512),
              arena[:, 5632:7680].bitcast(BF16).rearrange("p (k n) -> p k n", n=512)]
        X_tk = [Tk("X0"), Tk("X1")]
        arena_tks += X_tk

        def ada_buf(l, cg):
            if l == 0:
                i = cg % 4
                if i < 2:
                    return Xb[i], [X_tk[i]], "ada%d" % i
                return Wb[i - 2][:, :, 0:512], W_tk[i - 2][0:4], "ada%d" % i
            if cg in (0, 1):
                return Xb[cg], [X_tk[cg]], "ada%d" % cg
            if cg == 5:
                return Xb[0], [X_tk[0]], "ada0"
            j = cg - 2
            return hT[:, :, j * 512:(j + 1) * 512], [t for bb in range(4 * j, 4 * j + 4) for t in hT_tk[bb]], "ada%d" % cg

        def ada_load(l, cg):
            buf, tks, sem = ada_buf(l, cg)
            src = w_view(ada_w_d[l])[:, :, cg * 512:(cg + 1) * 512]
            LOAD("pool", sem, buf, src, tks)

        def adaln(l, preloaded=0):
            ptf, ptf_tk = next_pab()
            LOAD("sp", "c7", grow[:, :], ada_b_d[l, 2048:3072].partition_broadcast(NSEG), [grow_tk])
            nbuf = 4 if l == 0 else 5
            for cg in range(preloaded, nbuf):
                ada_load(l, cg)
            for cg in range(6):
                W, wtks, _ = ada_buf(l, cg)
                if cg < 4:
                    mms = []
                    for j in range(4):
                        fb = cg * 4 + j
                        for kc in range(8):
                            mms.append((ptf[:, fb * 6:fb * 6 + 6], W[:, kc, j * 128:(j + 1) * 128], scond_k[:, kc, :], kc == 0, kc == 7))
                    PE(mms, [ptf_tk], wtks + [scond_tk])
                else:
                    pr = Y
                    half = cg - 4
                    mms = [(pr[0:NSEG, half * 512:(half + 1) * 512], scond_k[:, kc, :], W[:, kc, 0:512], kc == 0, kc == 7) for kc in range(8)]
                    PE(mms, [Yh_tk[half]], wtks + [scond_tk])
                    TT("dve", grow[:, half * 512:(half + 1) * 512], pr[0:NSEG, half * 512:(half + 1) * 512],
                       grow[:, half * 512:(half + 1) * 512], ALU.add, [grow_tk], [Yh_tk[half], grow_tk])
                if cg + nbuf < 6:
                    ada_load(l, cg + nbuf)
            ptf3 = ptf[:, 0:96].rearrange("p (f s) -> p f s", s=NSEG)
            TT("dve", modT[:, :, :], ptf3, colsB[:, 16 * l:16 * l + 16].unsqueeze(2).to_broadcast([128, 16, NSEG]), ALU.add,
               [mod_tk], [ptf_tk, cols_tk])
            TS("dve", A_l[:, :, :], modT[:, 8:16, :], 1.0, None, ALU.add, None, [A_tk], [mod_tk])
            TT("dve", A_l[:, :, :], A_l[:, :, :], colsA[:, 48 + 8 * l:56 + 8 * l].unsqueeze(2).to_broadcast([128, 8, NSEG]), ALU.mult,
               [A_tk], [A_tk, cols_tk])

        st_tks = [Tk("st%d" % b) for b in range(NB)]
        xn2 = [xn_bf, arena[:, 7680:8192].bitcast(BF16)]
        xn2_tk = [xn_tk, Tk("xn1")]
        arena_tks.append(xn2_tk[1])

        def norm_mod(l):
            def stA(b):
                i = b % 2
                ACT(tmp[:, :], x_sb[:, b, :], AF.Square, [tmp_tk, st_tks[b]], [x_tk[b]], accum_out=stat[:, b:b + 1])
                ACT(stat[:, 16 + b:17 + b], stat[:, b:b + 1], AF.Sqrt, [st_tks[b]], [st_tks[b], eps_tk], scale=1.0 / D, bias=eps_ap)
                fw.op("dve", lambda e, b=b: e.reciprocal(out=stat[:, 32 + b:33 + b], in_=stat[:, 16 + b:17 + b]), outs=[st_tks[b]], ins=[st_tks[b]])
                TS("dve", xn2[i][:, :], x_sb[:, b, :], stat[:, 32 + b:33 + b], None, ALU.mult, None, [xn2_tk[i]], [x_tk[b], st_tks[b]])
                banks = [(PA, PAB_tk[0]), (PB, PAB_tk[1])] if i == 0 else [(O[0], O_tk[0]), (O[1], O_tk[1])]
                for hf in range(2):
                    pt, pt_tk = banks[hf]
                    mms = [(pt[:, j * 128:(j + 1) * 128], xn2[i][:, (hf * 4 + j) * 128:(hf * 4 + j + 1) * 128], identb[:, :], True, True) for j in range(4)]
                    PE(mms, [pt_tk], [xn2_tk[i], cb_tk])

            def stB(b):
                i = b % 2
                seg = b // 2
                banks = [(PA, PAB_tk[0]), (PB, PAB_tk[1])] if i == 0 else [(O[0], O_tk[0]), (O[1], O_tk[1])]
                for hf in range(2):
                    pt, pt_tk = banks[hf]
                    for j in range(4):
                        kc = hf * 4 + j
                        if hf == 0:
                            ACT(hT[:, kc, b * 128:(b + 1) * 128], pt[:, j * 128:(j + 1) * 128], AF.Identity, [hT_tk[b][kc]], [pt_tk, A_tk, mod_tk],
                                scale=A_l[:, kc, seg:seg + 1], bias=modT[:, kc, seg:seg + 1])
                        else:
                            TS("dve", hT[:, kc, b * 128:(b + 1) * 128], pt[:, j * 128:(j + 1) * 128], A_l[:, kc, seg:seg + 1], modT[:, kc, seg:seg + 1],
                               ALU.mult, ALU.add, [hT_tk[b][kc]], [pt_tk, A_tk, mod_tk])

            stA(0)
            for b in range(NB):
                if b + 1 < NB:
                    stA(b + 1)
                stB(b)

        eps_t = sb("eps_t", [128, 1])
        eps_tk = Tk("eps")
        fw.op("dve", lambda e: e.memset(eps_t[:, :], EPS), outs=[eps_tk])
        eps_ap = eps_t[:, 0:1]
        nhalf = sb("nhalf", [128, 16])
        nh_tk = Tk("nh")
        fw.op("dve", lambda e: e.memset(nhalf[:, :], -0.5), outs=[nh_tk])
        cst_t = sb("cst_t", [128, 2])
        cst_tk = Tk("cst")
        fw.op("dve", lambda e: e.memset(cst_t[:, 0:1], 1.0), outs=[cst_tk])
        fw.op("dve", lambda e: e.memset(cst_t[:, 1:2], 0.5), outs=[cst_tk])
        one_ap = cst_t[:, 0:1]
        half_ap = cst_t[:, 1:2]

        def proj_fm(W, wtk, c0, ncol, quad, pt, pt_tk, tok0=None, ntok=512):
            if tok0 is None:
                tok0 = quad * 512
            mms = [(pt[0:ncol, 0:ntok], W[:, kc, c0:c0 + ncol], hT[:, kc, tok0:tok0 + ntok], kc == 0, kc == 7) for kc in range(8)]
            PE(mms, [pt_tk], wsel(wtk, c0, ncol) + [t for bb in range(tok0 // 128, (tok0 + ntok) // 128) for t in hT_tk[bb]])

        def proj_v(W, wtk, c0, nv, b, vb):
            v_sb, v_tk = v_sbs[vb], v_tks[vb]
            pt, pt_tk = next_pab()
            mms = [(pt[:, 0:nv], hT[:, kc, b * 128:(b + 1) * 128], W[:, kc, c0:c0 + nv], kc == 0, kc == 7) for kc in range(8)]
            PE(mms, [pt_tk], wsel(wtk, c0, nv) + hT_tk[b])
            fw.op("act", lambda e: e.copy(out=v_sb[:, b, 0:nv], in_=pt[:, 0:nv]), outs=[v_tk[b]], ins=[pt_tk])

        maskT64 = [consts[:, C_MF:C_MF + 128], consts[:, C_MB:C_MB + 128]]
        state = [{"s32": 0, "sbf": 0}, {"s32": 0, "sbf": 0}]
        NSBF = 4

        def seg_start(d, seg, s0_src):
            stt = state[d]
            cur = stt["s32"] % 4
            nxt = (stt["s32"] + 1) % 4
            nb = (stt["sbf"] + 1) % NSBF
            first_seg = 0 if d == 0 else 3
            chained = (1 <= seg <= 3) if d == 0 else (0 <= seg <= 2)
            if seg == first_seg:
                LOAD("sp", "s32_%d_%d" % (d, nxt), S32[d][nxt][:, :], s0_src, [S32_tk[d][nxt]])
                fw.op("act", lambda e: e.copy(out=Sbf[d][nb][:, :], in_=S32[d][nxt][:, :]), outs=[Sbf_tk[d][nb]], ins=[S32_tk[d][nxt]])
            elif chained:
                TS("dve", S32[d][nxt][:, :], S32[d][cur][:, :], flag_sb[:, 0:1], None, ALU.mult, None, [S32_tk[d][nxt]], [S32_tk[d][cur], flag_tk])
                fw.op("act", lambda e: e.copy(out=Sbf[d][nb][:, :], in_=S32[d][nxt][:, :]), outs=[Sbf_tk[d][nb]], ins=[S32_tk[d][nxt]])
            else:
                fw.op("pool", lambda e: e.memset(S32[d][nxt][:, :], 0.0), outs=[S32_tk[d][nxt]])
                fw.op("pool", lambda e: e.memset(Sbf[d][nb][:, :], 0.0), outs=[Sbf_tk[d][nb]])
            stt["s32"] += 1
            stt["sbf"] += 1

        class Blk:
            def __init__(self, b, d, subs, qt, kt, ktr, qk_tks, lo, G_of, g_tks, st_dst, s0_src, first_visit, vb, ktok_scale=None,
                         nch=2, maskT=None, mask_tks=None):
                self.__dict__.update(locals())
                self.v_sb = v_sbs[vb]
                self.v_tk = v_tks[vb]
                self.nsub = len(subs)
                self.pi = b % 2
                self.maskT = maskT if maskT is not None else maskT64
                self.mask_tks = mask_tks if mask_tks is not None else [consts_tk]
                self.CH = 128 // nch
                if nch == 2:
                    self.order = [0, 1] if d == 0 else [1, 0]
                else:
                    self.order = [0]
                if self.nsub == 1 and nch == 2:
                    self.kvcol = [128, 384]
                else:
                    self.kvcol = [384, 384]
                self.sbf_idx = []

            def A(self):
                d, b, lo = self.d, self.b, self.lo
                Mb, Mtk = M[d], M_tk[d]
                ptok = None
                for j, (p0, dk, vc) in enumerate(self.subs):
                    ptok = PE([(Mb[:, j * 128:(j + 1) * 128], self.kt[p0:p0 + dk, lo:lo + 128], self.qt[p0:p0 + dk, lo:lo + 128], True, True)],
                              [Mtk], self.qk_tks, force=([ptok] if ptok else []))
                if self.nsub == 1:
                    pass
                PE([(Mb[:, 256:384], self.ktr[:, lo:lo + 128], identb[:, :], True, True)], [Mtk], self.qk_tks + [cb_tk])
                for j in range(self.nsub):
                    TT("dve", PT[d][self.pi][j][:, :], Mb[:, j * 128:(j + 1) * 128], self.maskT[d], ALU.mult, [PT_tk[d][self.pi][j]], [Mtk] + self.mask_tks)
                if self.ktok_scale is None:
                    fw.op("act", lambda e: e.copy(out=ktok[d][:, :], in_=Mb[:, 256:384]), outs=[ktok_tk[d]], ins=[Mtk])
                else:
                    ACT(ktok[d][:, :], Mb[:, 256:384], AF.Identity, [ktok_tk[d]], [Mtk] + self.g_tks, scale=self.ktok_scale)

            def kv(self, ci, force=()):
                d, b = self.d, self.b
                Mb, Mtk = M[d], M_tk[d]
                r0 = self.order[ci] * self.CH
                kc = self.kvcol[ci]
                mms = []
                for j, (p0, dk, vc) in enumerate(self.subs):
                    mms.append((Mb[p0:p0 + dk, kc:kc + 128], ktok[d][r0:r0 + self.CH, p0:p0 + dk], self.v_sb[r0:r0 + self.CH, b, vc:vc + 128], True, True))
                return PE(mms, [Mtk], [ktok_tk[d], self.v_tk[b]], force=force)

            def rec(self, ci):
                d = self.d
                stt = state[d]
                Mb, Mtk = M[d], M_tk[d]
                r0 = self.order[ci] * self.CH
                kc = self.kvcol[ci]
                cur = stt["s32"] % 4
                nxt = (stt["s32"] + 1) % 4
                if ci == 0:
                    self.sbf_idx.append(stt["sbf"] % NSBF)
                G = self.G_of(r0)
                STT(S32[d][nxt][:, :], S32[d][cur][:, :], G, Mb[:, kc:kc + 128], ALU.mult, ALU.add, [S32_tk[d][nxt]], [S32_tk[d][cur], Mtk] + self.g_tks)
                nbf = (stt["sbf"] + 1) % NSBF
                fw.op("act", lambda e, nbf=nbf, nxt=nxt: e.copy(out=Sbf[d][nbf][:, :], in_=S32[d][nxt][:, :]), outs=[Sbf_tk[d][nbf]], ins=[S32_tk[d][nxt]])
                self.sbf_idx.append(nbf)
                stt["s32"] += 1
                stt["sbf"] += 1

            def BC(self):
                d, b = self.d, self.b
                seg = b // 2
                at_start = (b % 2 == 0) if d == 0 else (b % 2 == 1)
                if at_start:
                    seg_start(d, seg, self.s0_src)
                n = len(self.order)
                if self.kvcol[0] != self.kvcol[1] or n == 1:
                    ptok = None
                    for ci in range(n):
                        ptok = self.kv(ci, force=([ptok] if ptok else []))
                    yield
                    for ci in range(n):
                        self.rec(ci)
                else:
                    for ci in range(n):
                        self.kv(ci)
                        yield
                        self.rec(ci)
                if not at_start:
                    cur = state[d]["s32"] % 4
                    STORE("sp", "s32_%d_%d" % (d, cur), self.st_dst(seg), S32[d][cur][:, :], [S32_tk[d][cur]])

            def D(self):
                d, b, lo = self.d, self.b, self.lo
                if self.nsub == 1:
                    obk = [OB[d][b % 2]]
                else:
                    obk = [OB[d][0], OB[d][1]]
                mms = []
                ins = list(self.qk_tks) + [self.v_tk[b]]
                for j, (p0, dk, vc) in enumerate(self.subs):
                    Ob = obk[j][0]
                    mms.append((Ob[:, 0:128], PT[d][self.pi][j][:, :], self.v_sb[:, b, vc:vc + 128], True, False))
                    ins.append(PT_tk[d][self.pi][j])
                for ci in range(len(self.order)):
                    r0 = self.order[ci] * self.CH
                    sbi = self.sbf_idx[ci]
                    ins.append(Sbf_tk[d][sbi])
                    for j, (p0, dk, vc) in enumerate(self.subs):
                        Ob = obk[j][0]
                        mms.append((Ob[r0:r0 + self.CH, 0:128], self.qt[p0:p0 + dk, lo + r0:lo + r0 + self.CH], Sbf[d][sbi][p0:p0 + dk, :], False, True))
                mms_sorted = [m for m in mms if m[3]] + [m for m in mms if not m[3]]
                PE(mms_sorted, [ob[1] for ob in obk], ins)
                for j in range(self.nsub):
                    Obj, Obj_tk = obk[j]
                    if b not in self.first_visit:
                        fw.op("act", lambda e, Obj=Obj, j=j: e.copy(out=o_sb[:, b, j * 128:(j + 1) * 128], in_=Obj[:, 0:128]), outs=[o_tk[b]], ins=[Obj_tk])
                    else:
                        TT("dve", o_sb[:, b, j * 128:(j + 1) * 128], Obj[:, 0:128], o_sb[:, b, j * 128:(j + 1) * 128], ALU.add, [o_tk[b]], [Obj_tk, o_tk[b]])
                self.first_visit.add(b)

        def scan_chain_gen(blks):
            prev = None
            for blk in blks:
                blk.A()
                yield
                if prev is not None:
                    prev.D()
                    yield
                yield from blk.BC()
                yield
                prev = blk
            prev.D()
            yield

        def run_interleaved(gens, weights=None):
            gens = list(gens)
            weights = list(weights) if weights else [1] * len(gens)
            alive = [True] * len(gens)
            while any(alive):
                for i, g in enumerate(gens):
                    for _ in range(weights[i]):
                        if not alive[i]:
                            break
                        try:
                            next(g)
                        except StopIteration:
                            alive[i] = False

        def chain(gs):
            for g in gs:
                yield from g

        ssq_tk = [Tk("ssq%d" % b) for b in range(NB)]

        def finalize(W, wtk, subs_fin, on_bf, on_tks):
            for (oc, gcol, ch, gncol) in subs_fin:
                TT("dve", on_bf[:, :, :], o_sb[:, :, oc:oc + 128], o_sb[:, :, oc:oc + 128], ALU.mult, on_tks, o_tk)
                yield
                fw.op("dve", lambda e: e.tensor_reduce(out=stat[:, 0:NB], in_=on_bf[:, :, :], axis=mybir.AxisListType.X, op=ALU.add),
                      outs=[stat_tk], ins=on_tks, cost=1.7)
                TS("dve", stat[:, 16:16 + NB], stat[:, 0:NB], 1.0 / 128, EPS, ALU.mult, ALU.add, [stat_tk], [stat_tk])
                yield
                TT("pool", stat[:, 32:32 + NB], stat[:, 16:16 + NB], nhalf[:, 0:NB], ALU.pow, [stat_tk], [stat_tk, nh_tk])
                yield
                TT("dve", on_bf[:, :, :], o_sb[:, :, oc:oc + 128], stat[:, 32:32 + NB].unsqueeze(2).to_broadcast([128, NB, 128]), ALU.mult,
                   on_tks, o_tk + [stat_tk])
                yield
                for quad in range(3):
                    pg, pg_tk = next_pab()
                    proj_fm(W, wtk, gcol, 128, quad, pg, pg_tk)
                    ACT(sg_q[:, :], pg[:, 0:512], AF.Silu, [sgq_tk], [pg_tk])
                    yield
                    pt, pt_tk = next_pab()
                    mms = [(pt[:, j * 128:(j + 1) * 128], on_bf[:, quad * 4 + j, :], identb[:, :], True, True) for j in range(4)]
                    PE(mms, [pt_tk], on_tks + [cb_tk])
                    STT(ogT[:, ch, quad * 512:(quad + 1) * 512], pt[:, 0:512], colsA[:, gncol:gncol + 1], sg_q[:, :], ALU.mult, ALU.mult,
                        [og_tk[ch]], [pt_tk, cols_tk, sgq_tk])
                    yield

        def gen_V(W, wtk, c0, nv, vb):
            for b in range(NB):
                proj_v(W, wtk, c0, nv, b, vb)
                yield

        def out_proj(l, w_out2d, last):
            fw.fence(arena_tks)
            if not last:
                for cg_ in range(5):
                    ada_load(1, cg_)
            if last:
                LOAD("sp", "c3", fnw_bc[:, :], fnw_d.partition_broadcast(128), [fnw_tk])
            for b in range(NB):
                seg = b // 2
                if b % 2 == 0:
                    for hf in range(2):
                        PE([(M[hf][:, 0:512], consts[0:NSEG, C_SEL + seg * 128:C_SEL + (seg + 1) * 128], grow[:, hf * 512:(hf + 1) * 512], True, True)],
                           [M_tk[hf]], [consts_tk, grow_tk])
                        fw.op("act", lambda e, hf=hf: e.copy(out=gbc[:, hf * 512:(hf + 1) * 512], in_=M[hf][:, 0:512]), outs=[gbc_tk], ins=[M_tk[hf]], cost=0.65)
                banks = OB[1] if b % 2 == 0 else OB[0]
                for hf in range(2):
                    bk, bk_tk = banks[hf]
                    mms = [(bk[:, 0:512], ogT[:, ch, b * 128:(b + 1) * 128], Wb[hf][:, ch, 0:512], ch == 0, ch == 7) for ch in range(8)]
                    PE(mms, [bk_tk], og_tk + W_tk[hf][0:4])
                    TT("dve", tmp[:, hf * 512:(hf + 1) * 512], bk[:, 0:512], gbc[:, hf * 512:(hf + 1) * 512], ALU.mult, [tmp_tk], [bk_tk, gbc_tk])
                TT("dve", x_sb[:, b, :], tmp[:, :], x_sb[:, b, :], ALU.add, [x_tk[b]], [tmp_tk, x_tk[b]])
                if last:
                    if b >= 1:
                        fin_norm(b - 1)
            if last:
                fin_norm(NB - 1)

        def fin_norm(b):
            ACT(xn_bf[:, :], x_sb[:, b, :], AF.Square, [xn_tk, st_tks[b]], [x_tk[b]], accum_out=stat[:, b:b + 1])
            ACT(stat[:, 16 + b:17 + b], stat[:, b:b + 1], AF.Sqrt, [st_tks[b]], [st_tks[b], eps_tk], scale=1.0 / D, bias=eps_ap)
            fw.op("dve", lambda e, b=b: e.reciprocal(out=stat[:, 32 + b:33 + b], in_=stat[:, 16 + b:17 + b]), outs=[st_tks[b]], ins=[st_tks[b]])
            STT(x_sb[:, b, :], x_sb[:, b, :], stat[:, 32 + b:33 + b], fnw_bc[:, :], ALU.mult, ALU.mult, [x_tk[b]], [x_tk[b], st_tks[b], fnw_tk])
            STORE("sp", "yout", y_d[b * 128:(b + 1) * 128, :], x_sb[:, b, :], [x_tk[b]])

        def ar(off_bytes, ncols, dt=F32):
            esz = 4 if dt == F32 else 2
            a_ = arena[:, off_bytes // 4:(off_bytes + ncols * esz) // 4]
            return a_ if dt == F32 else a_.bitcast(BF16)

        KB = 1024
        NH = 512
        scr = []
        scr_tk = []
        for d_ in range(2):
            base = d_ * 8 * KB
            scr.append(dict(sig=ar(base, NH), kk=ar(base + 2 * KB, NH), Ep=ar(base + 4 * KB, NH), Em=ar(base, NH),
                            q=ar(base + 6 * KB, NH), gl=ar(base + 4 * KB, NH), low=ar(base, NH, BF16)))
            tsig, tkk, tEp, tq = Tk("sig"), Tk("kk"), Tk("Ep"), Tk("q")
            scr_tk.append(dict(sig=tsig, kk=tkk, Ep=tEp, Em=tsig, q=tq, gl=tEp))
        Pq = [[[ar(16 * KB + ((d * 2 + i) * 3 + k) * KB, 512, BF16) for k in range(3)] for i in range(2)] for d in range(2)]
        Pq_tk = [[dict(qt=[Tk("qt"), Tk("qt")], kt=[Tk("kt"), Tk("kt")], kd=[Tk("kd") for c in range(8)]) for i in range(2)] for d in range(2)]
        Gq_tk = [[[Tk("Gq"), Tk("Gq")] for i in range(2)] for d in range(2)]
        on_bf0 = ar(28 * KB, T, BF16).rearrange("p (b v) -> p b v", v=128)
        on_tk0 = Tk("on0")
        arena_tks += [on_tk0]
        for d_ in range(2):
            arena_tks += list(scr_tk[d_].values())
            for i_ in range(2):
                arena_tks += Pq_tk[d_][i_]["qt"] + Pq_tk[d_][i_]["kt"] + Pq_tk[d_][i_]["kd"]
        rmask = [consts[:, C_SF:C_SF + 512], consts[:, C_SB:C_SB + 512]]
        PMIN = 1.8e-35

        try:
            checkpoint('setup')
            adaln(0)
            checkpoint('adaln0')
            load_hg_l0(0, 0)
            load_hg_l0(1, 1)
            norm_mod(0)
            checkpoint('norm0')

            def gates_l0(hg, W, wtk, d, quad, half, buf):
                is_h = hg < 4
                pr = hg - 4
                tok0 = quad * 512 + half * NH
                h0 = half * NH
                T_ = scr[d]
                K_ = scr_tk[d]
                sig_t, kk_t, Ep_t, Em_t, q_t, gl_t, low_t = T_["sig"], T_["kk"], T_["Ep"], T_["Em"], T_["q"], T_["gl"], T_["low"]
                qt_t, kt_t, kd_t = Pq[d][buf]
                ptk = Pq_tk[d][buf]
                if is_h:
                    pf, pf_tk = next_pab()
                    proj_fm(W, wtk, 256 + 128 * d, 128, quad, pf, pf_tk, tok0, NH)
                    ACT(sig_t, pf[:, 0:NH], AF.Tanh, [K_["sig"]], [pf_tk], scale=0.5)
                    yield
                    TS("dve", sig_t, sig_t, lbt[:, 4 + hg:5 + hg], lbt[:, 8 + hg:9 + hg], ALU.mult, ALU.add, [K_["sig"]], [K_["sig"], lb_tk])
                    ACT(kk_t, sig_t, AF.Identity, [K_["kk"]], [K_["sig"], cst_tk], scale=-1.0, bias=one_ap)
                    yield
                    pq, pq_tk = next_pab()
                    proj_fm(W, wtk, 0, 128, quad, pq, pq_tk, tok0, NH)
                    ACT(q_t, pq[:, 0:NH], AF.Silu, [K_["q"]], [pq_tk])
                else:
                    pl, pl_tk = next_pab()
                    proj_fm(W, wtk, 768 + 16 * d, 16, quad, pl, pl_tk, tok0, NH)
                    fw.op("act", lambda e, pl=pl: e.copy(out=low_t[0:16, :], in_=pl[0:16, 0:NH]), outs=[K_["Em"]], ins=[pl_tk], cost=0.45)
                    pz, pz_tk = next_pab()
                    PE([(pz[:, 0:NH], gkw_bf[0:16, d, pr * 128:(pr + 1) * 128], low_t[0:16, :], True, True)], [pz_tk], [gkw_tk, K_["Em"]])
                    ACT(gl_t, pz[:, 0:NH], AF.Tanh, [K_["gl"]], [pz_tk, lb_tk], scale=0.5, bias=lbt[:, 12 + 2 * d + pr:13 + 2 * d + pr])
                    yield
                    ACT(gl_t, gl_t, AF.Ln, [K_["gl"]], [K_["gl"], cst_tk], scale=0.5, bias=half_ap)
                    ACT(sig_t, gl_t, AF.Exp, [K_["sig"]], [K_["gl"]], scale=1.0 / 16.0)
                    pk, pk_tk = next_pab()
                    proj_fm(W, wtk, 128, 128, quad, pk, pk_tk, tok0, NH)
                    fw.op("act", lambda e, pk=pk: e.copy(out=kk_t, in_=pk[:, 0:NH]), outs=[K_["kk"]], ins=[pk_tk], cost=0.45)
                    yield
                    pq, pq_tk = next_pab()
                    proj_fm(W, wtk, 0, 128, quad, pq, pq_tk, tok0, NH)
                    ACT(q_t, pq[:, 0:NH], AF.Copy, [K_["q"]], [pq_tk], scale=0.125)
                yield
                if d == 0:
                    fw.op("dve", lambda e: e.tensor_tensor_scan(out=Ep_t, data0=rmask[0][:, 0:NH], data1=sig_t, initial=1.0, op0=ALU.max, op1=ALU.mult),
                          outs=[K_["Ep"]], ins=[K_["sig"], consts_tk], cost=0.7)
                else:
                    fw.op("dve", lambda e: e.tensor_tensor_scan(out=Ep_t[:, ::-1], data0=rmask[1][:, 0:NH][:, ::-1], data1=sig_t[:, ::-1], initial=1.0,
                                                                op0=ALU.max, op1=ALU.mult), outs=[K_["Ep"]], ins=[K_["sig"], consts_tk], cost=0.7)
                yield
                TS("dve", Em_t, Ep_t, PMIN, None, ALU.max, None, [K_["Em"]], [K_["Ep"]])
                fw.op("dve", lambda e: e.reciprocal(out=Em_t, in_=Em_t), outs=[K_["Em"]], ins=[K_["Em"]], cost=1.5)
                yield
                TT("dve", kk_t, kk_t, Em_t, ALU.mult, [K_["kk"]], [K_["kk"], K_["Em"]])
                yield
                fw.op("act", lambda e: e.copy(out=kt_t[:, h0:h0 + NH], in_=kk_t), outs=[ptk["kt"][half]], ins=[K_["kk"]], cost=0.65)
                TT("dve", qt_t[:, h0:h0 + NH], q_t, Ep_t, ALU.mult, [ptk["qt"][half]], [K_["q"], K_["Ep"]])
                Ep3 = Ep_t.rearrange("p (c t) -> p c t", t=64)
                gc = 63 if d == 0 else 0
                nck = NH // 64
                for cc in range(nck):
                    c = half * nck + cc
                    ACT(kd_t[:, c * 64:(c + 1) * 64], kk_t[:, cc * 64:(cc + 1) * 64], AF.Identity, [ptk["kd"][c]], [K_["kk"], K_["Ep"]],
                        scale=Ep_t[:, cc * 64 + gc:cc * 64 + gc + 1])
                    if cc % 2 == 1:
                        yield
                fw.op("dve", lambda e: e.tensor_copy(out=Gq[d][buf][:, half * nck:half * nck + nck], in_=Ep3[:, :, gc]), outs=[Gq_tk[d][buf][half]], ins=[K_["Ep"]])

            def hg_params(hg):
                slot = hg % 2
                is_h = hg < 4
                if is_h:
                    return Wb[slot], W_tk[slot], is_h, [(0, 128, 0)], 128, 128
                return Wb[slot], W_tk[slot], is_h, [(0, 64, 0), (64, 64, 128)], 256, 256

            def gates_dir(hg, step, d):
                W, wtk, is_h, subs, vcol, nv = hg_params(hg)
                nhalf_ = 512 // NH
                for half in (list(range(nhalf_)) if d == 0 else list(range(nhalf_))[::-1]):
                    yield from gates_l0(hg, W, wtk, d, [step, 2 - step][d], half, step % 2)

            def scan_chain(hg, step, d):
                W, wtk, is_h, subs, vcol, nv = hg_params(hg)
                pr = hg - 4
                buf = step % 2
                quad = [step, 2 - step][d]
                blks = []
                for i in range(4):
                    b = quad * 4 + (i if d == 0 else 3 - i)
                    lo = (b % 4) * 128
                    G_of = (lambda r0, d=d, buf=buf, lo=lo: Gq[d][buf][:, (lo + r0) // 64:(lo + r0) // 64 + 1])
                    if is_h:
                        st_dst = (lambda seg, d=d, hg=hg: sth_d[seg, d, hg])
                        s0_src = s0h_d[d, hg]
                    else:
                        st_dst = (lambda seg, d=d, pr=pr: stg_d[seg, d, 2 * pr:2 * pr + 2].rearrange("h k v -> (h k) v"))
                        s0_src = s0g_d[d, 2 * pr:2 * pr + 2].rearrange("h k v -> (h k) v")
                    first = o_seen[hg]
                    qt_t, kt_t, kd_t = Pq[d][buf]
                    ptk = Pq_tk[d][buf]
                    c0 = lo // 64
                    hf_ = lo // NH
                    blks.append(Blk(b, d, subs, qt_t, kt_t, kd_t, [ptk["qt"][hf_], ptk["kt"][hf_], ptk["kd"][c0], ptk["kd"][c0 + 1]], lo, G_of,
                                    [Gq_tk[d][buf][hf_]], st_dst, s0_src, first, hg % 2))
                return scan_chain_gen(blks)

            o_seen = {hg_: set() for hg_ in range(8)}

            def gen_V0(hg):
                W, wtk, is_h, subs, vcol, nv = hg_params(hg)
                return gen_V(W, wtk, vcol, nv, hg % 2)

            fw.fence(arena_tks)
            run_interleaved([gen_V0(0), gates_dir(0, 0, 0), gates_dir(0, 0, 1)])
            for hg in range(6):
                W, wtk, is_h, subs, vcol, nv = hg_params(hg)
                pr = hg - 4
                for step in range(3):
                    gens = [scan_chain(hg, step, 0), scan_chain(hg, step, 1)]
                    if step + 1 < 3:
                        gens.append(gates_dir(hg, step + 1, 0))
                        gens.append(gates_dir(hg, step + 1, 1))
                    if step == 0 and hg + 1 < 6:
                        gens.append(gen_V0(hg + 1))
                    run_interleaved(gens)
                    if step == 1 and hg + 2 < 6:
                        load_hg_l0(hg % 2, hg + 2, "early", fin_groups_l0(hg))
                if is_h:
                    fin = finalize(W, wtk, [(0, 512, hg, 64 + hg)], on_bf0, [on_tk0])
                else:
                    fin = finalize(W, wtk, [(0, 512, 4 + 2 * pr, 64 + 4 + 2 * pr), (128, 640, 5 + 2 * pr, 64 + 5 + 2 * pr)], on_bf0, [on_tk0])
                gens = [fin]
                if hg + 1 < 6:
                    gens.append(gates_dir(hg + 1, 0, 0))
                    gens.append(gates_dir(hg + 1, 0, 1))
                run_interleaved(gens)
                if hg + 2 < 6:
                    load_hg_l0(hg % 2, hg + 2, "late", fin_groups_l0(hg))
                else:
                    load_cols(hg % 2, 0, w_out_e_d[0], (hg % 2) * 512, 512, True)
                checkpoint('hg%d' % hg)
            out_proj(0, w_out_e_d[0], last=False)
            checkpoint('out0')

            adaln(1, preloaded=5)
            load_hg_l1(0, 0)
            load_hg_l1(1, 1)
            norm_mod(1)
            checkpoint('norm1')

            qk = [ar(i * 3 * KB, T, BF16) for i in range(4)]
            qk_tk = [Tk("qk%d" % i) for i in range(4)]
            xbf_b = [ar(12 * KB, 512, BF16), ar(27 * KB + 512, 512, BF16)]
            t1_b = [ar(13 * KB, 512), ar(28 * KB + 512, 512)]
            t2_b = [ar(15 * KB, 512), ar(30 * KB + 512, 512)]
            cs_t = [ar(17 * KB + j * 2 * KB, 512) for j in range(2)]
            EpEm = ar(21 * KB, 512)
            gam_t = ar(23 * KB, 128)
            m128 = ar(23 * KB + 512, 256)
            xbf_tks, t1_tks, t2_tks = [Tk("xbf"), Tk("xbf")], [Tk("t1"), Tk("t1")], [Tk("t2"), Tk("t2")]
            EpEm_tk, gam_tk, m128_tk = Tk("EpEm"), Tk("gam"), Tk("m128")
            on_bf1 = ar(24 * KB + 512, T, BF16).rearrange("p (b v) -> p b v", v=128)
            on_tk1 = Tk("on1")
            cs_tk = Tk("cs")
            arena_tks += qk_tk + xbf_tks + t1_tks + t2_tks + [EpEm_tk, gam_tk, m128_tk, on_tk1, cs_tk]
            fw.fence(arena_tks)
            ones128 = consts[:, C_CF:C_CF + 128]
            KSC = 128.0 ** -0.5
            LOAD("sp", "c8", m128, m128_d[:, :], [m128_tk])
            maskT128 = [m128[:, 0:128], m128[:, 128:256]]

            subs1 = [(0, 128, 0)]

            pcount = [0]

            def prologue1(h):
                slot = h % 2
                W = Wb[slot]
                wtk = W_tk[slot]
                for d in range(2):
                    TS("dve", gam_t, ones128, retd[:, 16 + d * 8 + h:17 + d * 8 + h], None, ALU.mult, None, [gam_tk], [consts_tk, retd_tk])
                    Epd = EpEm[:, d * 128:(d + 1) * 128]
                    if d == 0:
                        fw.op("dve", lambda e, Epd=Epd: e.tensor_tensor_scan(out=Epd, data0=gam_t, data1=ones128, initial=1.0, op0=ALU.mult, op1=ALU.min),
                              outs=[EpEm_tk], ins=[gam_tk, consts_tk])
                    else:
                        fw.op("dve", lambda e, Epd=Epd: e.tensor_tensor_scan(out=Epd[:, ::-1], data0=gam_t[:, ::-1], data1=ones128[:, ::-1], initial=1.0,
                                                                             op0=ALU.mult, op1=ALU.min), outs=[EpEm_tk], ins=[gam_tk, consts_tk])
                    fw.op("dve", lambda e, Epd=Epd, d=d: e.reciprocal(out=EpEm[:, 256 + d * 128:256 + (d + 1) * 128], in_=Epd), outs=[EpEm_tk], ins=[EpEm_tk])
                    TS("dve", EpEm[:, 256 + d * 128:256 + (d + 1) * 128], EpEm[:, 256 + d * 128:256 + (d + 1) * 128], KSC, None, ALU.mult, None, [EpEm_tk], [EpEm_tk])
                    yield
                for quad in range(3):
                    LOAD("sp", "cs", cs_t[0], cos_d[:, quad * 512:(quad + 1) * 512], [cs_tk])
                    LOAD("sp", "cs", cs_t[1], sin_d[:, quad * 512:(quad + 1) * 512], [cs_tk], fresh=False)
                    for which in range(2):
                        pi_ = pcount[0] % 2
                        pcount[0] += 1
                        xbf_t, t1_t, t2_t = xbf_b[pi_], t1_b[pi_], t2_b[pi_]
                        xbf_tk, t1_tk, t2_tk = xbf_tks[pi_], t1_tks[pi_], t2_tks[pi_]
                        pq, pq_tk = next_pab()
                        proj_fm(W, wtk, 128 * which, 128, quad, pq, pq_tk)
                        fw.op("act", lambda e, pq=pq, xbf_t=xbf_t: e.copy(out=xbf_t, in_=pq[:, 0:512]), outs=[xbf_tk], ins=[pq_tk], cost=0.65)
                        pr_, pr_tk = next_pab()
                        PE([(pr_[:, 0:512], Rb[:, :], xbf_t, True, True)], [pr_tk], [cb_tk, xbf_tk])
                        TT("dve", t1_t, pq[:, 0:512], cs_t[0], ALU.mult, [t1_tk], [pq_tk, cs_tk])
                        TT("dve", t2_t, pr_[:, 0:512], cs_t[1], ALU.mult, [t2_tk], [pr_tk, cs_tk])
                        yield
                        TT("dve", t1_t, t1_t, t2_t, ALU.add, [t1_tk], [t1_tk, t2_tk])
                        yield
                        t1v = t1_t.rearrange("p (c t) -> p c t", t=128)
                        for d in range(2):
                            dst = qk[2 * d + which][:, quad * 512:(quad + 1) * 512].rearrange("p (c t) -> p c t", t=128)
                            if which == 0:
                                e_bc = EpEm[:, d * 128:(d + 1) * 128].unsqueeze(1).to_broadcast([128, 4, 128])
                                TT("pool" if d == 0 else "dve", dst, t1v, e_bc, ALU.mult, [qk_tk[2 * d + which]], [t1_tk, EpEm_tk])
                            else:
                                e_bc = EpEm[:, 256 + d * 128:256 + (d + 1) * 128].unsqueeze(1).to_broadcast([128, 4, 128])
                                TT("dve" if d == 0 else "pool", dst, t1v, e_bc, ALU.mult, [qk_tk[2 * d + which]], [t1_tk, EpEm_tk])
                        yield

            def scan_chain1(h, d):
                blks = []
                for i in range(NB):
                    b = i if d == 0 else NB - 1 - i
                    gcol = 127 if d == 0 else 128
                    G_of = (lambda r0, gcol=gcol: EpEm[:, gcol:gcol + 1])
                    st_dst = (lambda seg, d=d, h=h: str_d[seg, d, h])
                    first = o_seen1[h]
                    blks.append(Blk(b, d, subs1, qk[2 * d], qk[2 * d + 1], qk[2 * d + 1], [qk_tk[2 * d], qk_tk[2 * d + 1]], b * 128, G_of, [EpEm_tk],
                                    st_dst, s0r_d[d, h], first, h % 2, ktok_scale=EpEm[:, gcol:gcol + 1], nch=1, maskT=maskT128, mask_tks=[m128_tk]))
                return scan_chain_gen(blks)

            o_seen1 = {h_: set() for h_ in range(8)}

            def gen_V1(h):
                return gen_V(Wb[h % 2], W_tk[h % 2], 256, 128, h % 2)

            run_interleaved([chain([gen_V1(0), prologue1(0)])])
            for h in range(8):
                slot = h % 2
                W = Wb[slot]
                wtk = W_tk[slot]
                if h + 2 < 8:
                    load_hg_l1(slot, h + 2, "early")
                gens = [scan_chain1(h, 0), scan_chain1(h, 1)]
                if h + 1 < 8:
                    gens.append(gen_V1(h + 1))
                run_interleaved(gens, [1, 1, 2][:len(gens)])
                gens = [finalize(W, wtk, [(0, 384, h, 72 + h)], on_bf1, [on_tk1])]
                if h + 1 < 8:
                    gens.append(prologue1(h + 1))
                run_interleaved(gens, [1, 3][:len(gens)])
                if h + 2 < 8:
                    load_hg_l1(slot, h + 2, "late")
                else:
                    load_cols(slot, 0, w_out_o_d[0], slot * 512, 512, True)
                checkpoint('rh%d' % h)
            out_proj(1, w_out_o_d[0], last=True)
        except _Stop:
            pass

        fw.wait_all("sp", [(k, fw.dma_cnt[k]) for k in fw.dma_cnt])
        print("recorded instructions:", fw.n_ins)
        fw.replay()
    return nc


def _consts():
    c = np.zeros((128, NCONST), np.float32)
    c[:, C_ID:C_ID + 128] = np.eye(128, dtype=np.float32)
    s = np.arange(128)[:, None]
    t = np.arange(128)[None, :]
    same = (s // 64) == (t // 64)
    c[:, C_MF:C_MF + 128] = (same & (s <= t)).astype(np.float32)
    c[:, C_MB:C_MB + 128] = (same & (s >= t)).astype(np.float32)
    R = np.zeros((128, 128), np.float32)
    for j in range(32):
        R[32 + j, j] = -1.0
        R[j, 32 + j] = 1.0
        R[96 + j, 64 + j] = -1.0
        R[64 + j, 96 + j] = 1.0
    c[:, C_R:C_R + 128] = R
    tt = np.arange(512)
    c[:, C_SF:C_SF + 512] = (tt % 64 == 0).astype(np.float32)[None, :]
    c[:, C_SB:C_SB + 512] = (tt % 64 == 63).astype(np.float32)[None, :]
    c[:, C_CF:C_CF + 128] = 1.0
    for seg in range(NSEG):
        c[seg, C_SEL + seg * 128:C_SEL + (seg + 1) * 128] = 1.0
    return c


def _rope_tables():
    tpos = np.arange(1024)
    row = (tpos // 64).astype(np.float32)
    col = (tpos % 64).astype(np.float32)
    half = 64
    inv = (10000.0 ** (-np.arange(0, half, 2, dtype=np.float32) / half)).astype(np.float32)
    ang_r = row[:, None] * inv[None, :]
    ang_c = col[:, None] * inv[None, :]
    ang = np.concatenate([ang_r, ang_r, ang_c, ang_c], axis=-1)
    return np.cos(ang).astype(np.float32).T.copy(), np.sin(ang).astype(np.float32).T.copy()


_PROGRAM = None


def _core_layout():
    lay = []
    for c in range(4):
        lay.append([("s", c, j) for j in range(4)] + [("p", 2 * c), ("p", 2 * c + 1)])
    for c in range(4, 8):
        lay.append([("p", 8 + 6 * (c - 4) + j) for j in range(6)])
    return lay


def kernel(x_prompt, x_sample, state_hgrn, state_gla, state_ret, c, c_ctx, norm_w, ada_w, ada_b,
           w_in_even, hgrn_lb, gla_gk_w, gla_gk_b, gn_even, w_out_even, w_in_odd, ret_decay, gn_odd,
           w_out_odd, final_norm_w):
    global _PROGRAM
    f32 = lambda a: np.ascontiguousarray(np.asarray(a, dtype=np.float32))
    x_prompt, x_sample = f32(x_prompt), f32(x_sample)
    state_hgrn, state_gla, state_ret = f32(state_hgrn), f32(state_gla), f32(state_ret)
    c, c_ctx = f32(c), f32(c_ctx)
    if _PROGRAM is None:
        _PROGRAM = build_program()
    nc = _PROGRAM
    consts = _consts()
    cosT, sinT = _rope_tables()
    lay = _core_layout()
    s_ = np.arange(128)[:, None]
    t_ = np.arange(128)[None, :]
    m128 = np.concatenate([(s_ <= t_).astype(np.float32), (s_ >= t_).astype(np.float32)], axis=1)
    shared = dict(consts=consts, m128=m128, norm_w=f32(norm_w), ada_w=f32(ada_w), ada_b=f32(ada_b), w_in_even=f32(w_in_even),
                  hgrn_lb=f32(hgrn_lb), gla_gk_w=f32(gla_gk_w), gla_gk_b=f32(gla_gk_b), gn_even=f32(gn_even),
                  w_out_even=f32(w_out_even), w_in_odd=f32(w_in_odd), ret_decay=f32(ret_decay), gn_odd=f32(gn_odd),
                  w_out_odd=f32(w_out_odd), final_norm_w=f32(final_norm_w))
    in_maps = []
    for ci in range(NCORES):
        xs = np.empty((T, D), np.float32)
        cond = np.empty((NSEG, D), np.float32)
        cs = np.ones((128, T), np.float32)
        sn = np.zeros((128, T), np.float32)
        for si, ent in enumerate(lay[ci]):
            if ent[0] == "s":
                xs[si * 256:(si + 1) * 256] = x_sample[ent[1], ent[2] * 256:(ent[2] + 1) * 256]
                cond[si] = c[ent[1]]
                cs[:, si * 256:(si + 1) * 256] = cosT[:, ent[2] * 256:(ent[2] + 1) * 256]
                sn[:, si * 256:(si + 1) * 256] = sinT[:, ent[2] * 256:(ent[2] + 1) * 256]
            else:
                xs[si * 256:(si + 1) * 256] = x_prompt[ent[1]]
                cond[si] = c_ctx
        if ci < 4:
            flag = np.ones((128, 1), np.float32)
            s0h = state_hgrn[ci, 0]
            s0g = state_gla[ci, 0]
            s0r = state_ret[ci, 0]
        else:
            flag = np.zeros((128, 1), np.float32)
            s0h = np.zeros((2, 4, 128, 128), np.float32)
            s0g = np.zeros((2, 4, 64, 128), np.float32)
            s0r = np.zeros((2, 8, 128, 128), np.float32)
        m = dict(shared)
        m.update(x=xs, cond=cond, flag=flag, s0h=np.ascontiguousarray(s0h), s0g=np.ascontiguousarray(s0g),
                 s0r=np.ascontiguousarray(s0r), cosT=cs, sinT=sn)
        in_maps.append(m)
    res = run_bass_kernel_spmd(nc, in_maps, core_ids=list(range(NCORES)))
    outs = res.results
    B = x_prompt.shape[0]
    y_prompt = np.empty((B, 256, D), np.float32)
    y_sample = np.empty((x_sample.shape[0], 1024, D), np.float32)
    nsh = np.empty((B, 1, 2, 4, 128, 128), np.float32)
    nsg = np.empty((B, 1, 2, 4, 64, 128), np.float32)
    nsr = np.empty((B, 1, 2, 8, 128, 128), np.float32)
    for ci in range(NCORES):
        r = outs[ci]
        for si, ent in enumerate(lay[ci]):
            yb = r["y"][si * 256:(si + 1) * 256]
            if ent[0] == "s":
                y_sample[ent[1], ent[2] * 256:(ent[2] + 1) * 256] = yb
            else:
                y_prompt[ent[1]] = yb
                nsh[ent[1], 0] = r["st_h"][si]
                nsg[ent[1], 0] = r["st_g"][si]
                nsr[ent[1], 0] = r["st_r"][si]
    return (y_prompt, y_sample, nsh, nsg, nsr)
```
